# Optimizing a Trainium2 kernel written in Bass

```python
import jax, jax.numpy as jnp
from jax import lax
import numpy as np

D_MODEL = 1024
BATCH = 4
SEQ = 8192
DEPTH = 1

CHUNK = 64
N_META = 16
D_MIX = D_MODEL
D_ATTN = D_MIX // 2
HEAD_DIM = 64
N_HEADS = D_ATTN // HEAD_DIM
D_POOL = D_MIX - D_ATTN
POOL_WINDOWS = (2, 4, 8, 16)
N_POOL_GROUPS = len(POOL_WINDOWS)
POOL_GROUP_DIM = D_POOL // N_POOL_GROUPS
D_IN = 3 * D_ATTN + N_HEADS + D_POOL
D_FF = ((8 * D_MODEL + 3 * 256 - 1) // (3 * 256)) * 256
Q_BLOCK = 128
EPS = 1e-6

kernel_name = "hymba_fox_poolformer_block"


def _rmsnorm(x, w):
    x32 = x.astype(jnp.float32)
    y = x32 * lax.rsqrt(jnp.mean(x32 * x32, axis=-1, keepdims=True) + EPS)
    return (y * w.astype(jnp.float32)).astype(x.dtype)


def _forgetting_attention(q, k, v, cum_logf):
    L = q.shape[2]
    scale = HEAD_DIM ** -0.5
    outs = []
    for q0 in range(0, L, Q_BLOCK):
        q1 = min(q0 + Q_BLOCK, L)
        qb = q[:, :, q0:q1]
        kp = k[:, :, :q1]
        vp = v[:, :, :q1]
        s = jnp.einsum("bhqd,bhkd->bhqk", qb, kp).astype(jnp.float32) * scale
        s = s + (cum_logf[:, :, q0:q1, None] - cum_logf[:, :, None, :q1])
        t_idx = jnp.arange(q0, q1)[:, None]
        s_idx = jnp.arange(q1)[None, :]
        s = jnp.where(s_idx <= t_idx, s, -jnp.inf)
        p = jax.nn.softmax(s, axis=-1).astype(v.dtype)
        outs.append(jnp.einsum("bhqk,bhkd->bhqd", p, vp))
    return jnp.concatenate(outs, axis=2)


def _trailing_mean_minus_self(u, window):
    L = u.shape[1]
    cs = jnp.concatenate([jnp.zeros_like(u[:, :1]), jnp.cumsum(u, axis=1)], axis=1)
    t = jnp.arange(L)
    lo = jnp.maximum(t + 1 - window, 0)
    total = cs[:, 1:] - cs[:, lo]
    count = (t + 1 - lo).astype(jnp.float32)[None, :, None]
    return total / count - u


def _layer(h, norm1_w, w_in, b_fgate, q_norm_w, k_norm_w, w_pool, pool_scale,
           w_out, norm2_w, w_gate, w_up, w_down):
    B, L, _ = h.shape
    n1 = _rmsnorm(h, norm1_w)
    proj = n1 @ w_in.astype(h.dtype)
    q, k, v, fg, u = jnp.split(
        proj, [D_ATTN, 2 * D_ATTN, 3 * D_ATTN, 3 * D_ATTN + N_HEADS], axis=-1)

    q = _rmsnorm(q.reshape(B, L, N_HEADS, HEAD_DIM), q_norm_w).transpose(0, 2, 1, 3)
    k = _rmsnorm(k.reshape(B, L, N_HEADS, HEAD_DIM), k_norm_w).transpose(0, 2, 1, 3)
    v = v.reshape(B, L, N_HEADS, HEAD_DIM).transpose(0, 2, 1, 3)
    log_f = jax.nn.log_sigmoid((fg + b_fgate.astype(fg.dtype)).astype(jnp.float32))
    cum_logf = jnp.cumsum(log_f.transpose(0, 2, 1), axis=-1)
    a_out = _forgetting_attention(q, k, v, cum_logf)
    a_out = a_out.transpose(0, 2, 1, 3).reshape(B, L, D_ATTN)

    u32 = u.astype(jnp.float32)
    pooled = jnp.stack(
        [_trailing_mean_minus_self(u32[..., g * POOL_GROUP_DIM:(g + 1) * POOL_GROUP_DIM], w)
         for g, w in enumerate(POOL_WINDOWS)], axis=2).astype(h.dtype)
    p_out = jnp.einsum("blgc,gcd->blgd", pooled, w_pool.astype(h.dtype)).reshape(B, L, D_POOL)
    p_out = p_out * pool_scale.astype(h.dtype)

    mix = jnp.concatenate([a_out, p_out], axis=-1) @ w_out.astype(h.dtype)
    h = h + mix

    n2 = _rmsnorm(h, norm2_w)
    ffn = (jax.nn.silu(n2 @ w_gate.astype(h.dtype)) * (n2 @ w_up.astype(h.dtype))) @ w_down.astype(h.dtype)
    return h + ffn


def setup_inputs(seed: int = 0) -> dict:
    key = jax.random.key(seed)
    ks = jax.random.split(key, 16)
    x = jax.random.normal(ks[0], (BATCH, SEQ, D_MODEL), jnp.float32)
    meta_tokens = jax.random.normal(ks[1], (N_META, D_MODEL), jnp.float32)
    norm1_w = 1.0 + 0.05 * jax.random.normal(ks[2], (DEPTH, D_MODEL), jnp.float32)
    w_in = jax.random.normal(ks[3], (DEPTH, D_MODEL, D_IN), jnp.float32) * D_MODEL ** -0.5
    w_in = w_in.at[:, :, 3 * D_ATTN:3 * D_ATTN + N_HEADS].multiply(0.1)
    b_fgate = 2.0 + 2.0 * jax.random.uniform(ks[4], (DEPTH, N_HEADS), jnp.float32)
    q_norm_w = 1.0 + 0.05 * jax.random.normal(ks[5], (DEPTH, HEAD_DIM), jnp.float32)
    k_norm_w = 1.0 + 0.05 * jax.random.normal(ks[6], (DEPTH, HEAD_DIM), jnp.float32)
    w_pool = jax.random.normal(ks[7], (DEPTH, N_POOL_GROUPS, POOL_GROUP_DIM, POOL_GROUP_DIM),
                               jnp.float32) * POOL_GROUP_DIM ** -0.5
    pool_scale = 1.0 + 0.1 * jax.random.normal(ks[8], (DEPTH, D_POOL), jnp.float32)
    w_out = jax.random.normal(ks[9], (DEPTH, D_MIX, D_MODEL), jnp.float32) * D_MIX ** -0.5
    norm2_w = 1.0 + 0.05 * jax.random.normal(ks[10], (DEPTH, D_MODEL), jnp.float32)
    w_gate = jax.random.normal(ks[11], (DEPTH, D_MODEL, D_FF), jnp.float32) * D_MODEL ** -0.5
    w_up = jax.random.normal(ks[12], (DEPTH, D_MODEL, D_FF), jnp.float32) * D_MODEL ** -0.5
    w_down = jax.random.normal(ks[13], (DEPTH, D_FF, D_MODEL), jnp.float32) * D_FF ** -0.5
    return {"x": x, "meta_tokens": meta_tokens, "norm1_w": norm1_w, "w_in": w_in,
            "b_fgate": b_fgate, "q_norm_w": q_norm_w, "k_norm_w": k_norm_w,
            "w_pool": w_pool, "pool_scale": pool_scale, "w_out": w_out,
            "norm2_w": norm2_w, "w_gate": w_gate, "w_up": w_up, "w_down": w_down}


def reference(x, meta_tokens, norm1_w, w_in, b_fgate, q_norm_w, k_norm_w, w_pool,
              pool_scale, w_out, norm2_w, w_gate, w_up, w_down):
    B = x.shape[0]
    meta = jnp.broadcast_to(meta_tokens.astype(x.dtype)[None], (B, N_META, D_MODEL))
    h = jnp.concatenate([meta, x], axis=1)
    for layer in range(DEPTH):
        h = _layer(h, norm1_w[layer], w_in[layer], b_fgate[layer], q_norm_w[layer],
                   k_norm_w[layer], w_pool[layer], pool_scale[layer], w_out[layer],
                   norm2_w[layer], w_gate[layer], w_up[layer], w_down[layer])
    return h[:, N_META:]
```

```python
import numpy as np
from contextlib import ExitStack
import concourse.bass as bass
import concourse.mybir as mybir
from concourse.bass_utils import run_bass_kernel_spmd

F32 = mybir.dt.float32
BF16 = mybir.dt.bfloat16
AF = mybir.ActivationFunctionType
ALU = mybir.AluOpType

D = 1024
NH = 8
DFF = 2816
NF = DFF // 128
EPS = 1e-6
XR = 6


class Res:
    __slots__ = ("w", "r", "dsem", "dcnt", "name", "excl")

    def __init__(self, name="", excl=False):
        self.excl = excl
        self.w = None
        self.r = {}
        self.dsem = None
        self.dcnt = 0
        self.name = name


class Tracker:
    def __init__(self, nc, es):
        self.nc = nc
        self.es = es
        self.eng = dict(pe=nc.tensor, act=nc.scalar, dve=nc.vector, pool=nc.gpsimd, sp=nc.sync)
        self.sem = {k: es.enter_context(nc.semaphore("s_" + k)) for k in self.eng}
        self.cnt = {k: 0 for k in self.eng}
        self.seen = {k: {} for k in self.eng}
        self.pend = {k: ([], []) for k in self.eng}
        self.dres = []
        self.nd = 0

    def _wait(self, e, toks):
        for tk in toks:
            if tk is None:
                continue
            sem, val = tk
            if self.seen[e].get(sem.name, 0) >= val:
                continue
            self.eng[e].wait_ge(sem, val)
            self.seen[e][sem.name] = val

    def _deps(self, e, reads, writes):
        toks = []
        for e2, (pr, pw) in self.pend.items():
            if e2 == e:
                continue
            for r in reads:
                assert all(r is not x for x in pw), "read of pending write %s" % r.name
            for w in writes:
                assert all(w is not x for x in pw) and all(w is not x for x in pr), \
                    "write of pending %s" % w.name
        for r in reads:
            toks.append(r.w)
            if r.excl:
                own = self.sem[e].name
                toks.extend(tk for nm, tk in r.r.items() if nm != own)
        for w in writes:
            toks.append(w.w)
            toks.extend(w.r.values())
        return toks

    def op(self, e, fn, reads=(), writes=(), sig=True, extra=()):
        self._wait(e, self._deps(e, reads, writes) + list(extra))
        ins = fn()
        pr, pw = self.pend[e]
        pr.extend(reads)
        pw.extend(writes)
        if sig:
            self.cnt[e] += 1
            ins.then_inc(self.sem[e], 1)
            tk = (self.sem[e], self.cnt[e])
            for r in pr:
                r.r[self.sem[e].name] = tk
            for w in pw:
                w.w = tk
                w.r = {}
            self.pend[e] = ([], [])
            return tk
        return None

    def dma(self, q, out, in_, reads=(), writes=(), extra=()):
        self._wait(q, self._deps(q, reads, writes) + list(extra))
        res = writes[0] if writes else reads[0]
        if res.dsem is None:
            res.dsem = self.es.enter_context(self.nc.semaphore("d%d" % self.nd))
            self.nd += 1
            self.dres.append(res)
        res.dcnt += 16
        self.eng[q].dma_start(out=out, in_=in_).then_inc(res.dsem, 16)
        tk = (res.dsem, res.dcnt)
        for r in reads:
            r.r[res.dsem.name] = tk
        for w in writes:
            w.w = tk
            w.r = {}
        return tk

    def barrier(self):
        for e in self.eng:
            assert not self.pend[e][0] and not self.pend[e][1], "pending at barrier"
        toks = [(self.sem[e], self.cnt[e]) for e in self.eng if self.cnt[e] > 0]
        toks += [(r.dsem, r.dcnt) for r in self.dres]
        for e in self.eng:
            self._wait(e, toks)


class _Stop(Exception):
    pass


def build(nslot, stop=None):
    try:
        return _build(nslot, stop)
    except _Stop as e:
        return e.args[0]


def _build(nslot, stop=None):
    NV = 8 * nslot + 1
    NOWN = 4 * nslot
    NVC = NV * 128
    NG = (NV + 3) // 4
    nc = bass.Bass("TRN2", target_bir_lowering=False)

    def din(name, shape):
        return nc.dram_tensor(name, shape, F32, kind="ExternalInput").ap()

    xv = din("xv", [NVC, D])
    flag_d = din("flag", [128, NV])
    w_in = din("w_in", [D, 2056])
    w_out = din("w_out", [D, D])
    w_gate = din("w_gate", [D, DFF])
    w_up = din("w_up", [D, DFF])
    w_down = din("w_down", [DFF, D])
    w_pool = din("w_pool", [4, 128, 128])
    n1w_d = din("n1w", [128, 8])
    n2w_d = din("n2w", [128, 8])
    qw2_d = din("qw2", [128, 1])
    kw2_d = din("kw2", [128, 1])
    bfg_d = din("bfg", [128, 8])
    psc_d = din("psc", [128, 4])
    out_d = nc.dram_tensor("out", [NOWN * 128, D], F32, kind="ExternalOutput").ap()
    wu_s = nc.dram_tensor("wu_s", [D, 512], BF16, kind="Internal").ap()
    wo_s = nc.dram_tensor("wo_s", [D, D], BF16, kind="Internal").ap()
    wg_s = nc.dram_tensor("wg_s", [D, DFF], BF16, kind="Internal").ap()
    wp_s = nc.dram_tensor("wp_s", [D, DFF], BF16, kind="Internal").ap()
    wd_s = nc.dram_tensor("wd_s", [DFF, D], BF16, kind="Internal").ap()

    with ExitStack() as es:
        T = Tracker(nc, es)

        def stop_here(tag):
            if stop == tag:
                T.barrier()
                raise _Stop(nc)

        def sb(name, shape, dtype, st=None):
            return (st or es).enter_context(nc.sbuf_tensor(name, shape, dtype))

        ident = sb("ident", [128, 128], BF16)
        tri = sb("tri", [128, 128], BF16)
        bones = sb("bones", [128, 128], BF16)
        trif = sb("trif", [128, 128], F32)
        onesf = sb("onesf", [128, 128], F32)
        n1w = sb("n1w_sb", [128, 8], F32)
        n2w = sb("n2w_sb", [128, 8], F32)
        qw2 = sb("qw2_sb", [128, 1], F32)
        kw2 = sb("kw2_sb", [128, 1], F32)
        bfg = sb("bfg_sb", [128, 8], F32)
        psc = sb("psc_sb", [128, 4], F32)
        flagS = sb("flagS", [128, NV], F32)
        AT = sb("AT", [128, 4, NOWN * 128], BF16)
        PB = [es.enter_context(nc.psum_tensor("pb%d" % i, [128, 512], F32)) for i in range(8)]
        PBb = [p.bitcast(BF16) for p in PB]
        R_PB = [Res("pb%d" % i, excl=True) for i in range(8)]
        R_c = Res("consts")

        T.op("pool", lambda: nc.gpsimd.memset(ident[:], 1.0), writes=[R_c])
        T.op("pool", lambda: nc.gpsimd.affine_select(
            out=ident[:], in_=ident[:], pattern=[[1, 128]], compare_op=ALU.is_equal,
            fill=0.0, base=0, channel_multiplier=-1), reads=[R_c], writes=[R_c])
        T.op("pool", lambda: nc.gpsimd.memset(tri[:], 1.0))
        T.op("pool", lambda: nc.gpsimd.memset(trif[:], 1.0))
        T.op("pool", lambda: nc.gpsimd.memset(onesf[:], 1.0))
        T.op("pool", lambda: nc.gpsimd.memset(bones[:], 0.0), writes=[R_c])
        T.op("pool", lambda: nc.gpsimd.memset(bones[0:64, 0:64], 1.0), reads=[R_c], writes=[R_c])
        T.op("pool", lambda: nc.gpsimd.memset(bones[64:128, 64:128], 1.0), reads=[R_c], writes=[R_c])
        T.op("pool", lambda: nc.gpsimd.affine_select(
            out=tri[:], in_=tri[:], pattern=[[1, 128]], compare_op=ALU.is_ge,
            fill=0.0, base=0, channel_multiplier=-1), reads=[R_c], writes=[R_c])
        T.op("pool", lambda: nc.gpsimd.affine_select(
            out=trif[:], in_=trif[:], pattern=[[1, 128]], compare_op=ALU.is_ge,
            fill=0.0, base=0, channel_multiplier=-1), reads=[R_c], writes=[R_c])
        R_v = Res("vecs")
        for dst, src in ((n1w, n1w_d), (n2w, n2w_d), (qw2, qw2_d), (kw2, kw2_d),
                         (bfg, bfg_d), (psc, psc_d), (flagS, flag_d)):
            T.dma("sp", dst[:], src, writes=[R_v])
            R_v.w = None

        st_att = ExitStack()
        KT = sb("KT", [128, 2, NVC], BF16, st_att)
        VAf = sb("VAf", [128, NV * 260 + 64], BF16, st_att)
        VA = VAf[:, 0:NV * 260].rearrange("p (t h c) -> p t h c", t=NV, h=4)
        atmp = [sb("atmp%d" % i, [64, 512], BF16, st_att) for i in range(2)]
        R_atmp = [Res("atmp0"), Res("atmp1")]
        Wk = sb("Wk", [128, 8, 256], BF16, st_att)
        Wq = sb("Wq", [128, 8, 256], BF16, st_att)
        Wvf = sb("Wvf", [128, 8, 260], BF16, st_att)
        FG = sb("FG", [128, NV * 4], F32, st_att)
        EL = sb("EL", [128, NV * 4], F32, st_att)
        TOT = sb("TOT", [128, NV * 4], F32, st_att)
        PFX = sb("PFX", [128, (NV + 1) * 4], F32, st_att)
        CUMP = sb("CUMP", [128, NV * 4], F32, st_att)
        BIASJ = [sb("BIASJ%d" % i, [128, NV], F32, st_att) for i in range(2)]
        xs = [sb("xs%d" % i, [128, D], F32, st_att) for i in range(XR)]
        xn = sb("xn", [128, 4, D], BF16, st_att)
        xnB = sb("xnB", [128, 4, D], BF16, st_att)
        nT = [sb("nT%d" % i, [128, 8, 512], BF16, st_att) for i in range(2)]
        ssq = sb("ssq", [128, 4], F32, st_att)
        rs = sb("rs", [128, 4], F32, st_att)
        sqb = [sb("sqb%d" % i, [128, 512], BF16, st_att) for i in range(2)]
        rstd = [sb("rstd%d" % i, [128, 512], F32, st_att) for i in range(2)]
        junk = [sb("junk%d" % i, [128, D], BF16, st_att) for i in range(2)]
        QT = [sb("QT%d" % i, [128, 4, 512], BF16, st_att) for i in range(2)]
        NPT = 4
        PT = [sb("PT%d" % i, [128, 512], BF16, st_att) for i in range(NPT)]
        rec = sb("rec", [128, 512], F32, st_att)
        bcs = sb("bcs", [128, 512], F32, st_att)

        R_xs = [Res("xs%d" % i) for i in range(XR)]
        R_xn = [Res("xn%d" % i) for i in range(4)]
        R_xnB = [Res("xnB%d" % i) for i in range(4)]
        xsel = {"xn": xn, "R": R_xn}
        R_nT = [[Res("nT%d_%d" % (b, k)) for k in range(8)] for b in range(2)]
        R_ssq, R_rs, R_junk = Res("ssq"), Res("rs"), [Res("junk0"), Res("junk1")]
        R_sq = [Res("sq0"), Res("sq1")]
        R_rstd = [Res("rstd0"), Res("rstd1")]
        R_Wk, R_Wq, R_Wvf = Res("Wk"), Res("Wq"), Res("Wvf")
        R_QT = [Res("QT0"), Res("QT1")]
        R_PT = [Res("PT%d" % i) for i in range(NPT)]
        R_BJ = [Res("BJ0"), Res("BJ1")]
        R_rec, R_bcs = Res("rec"), Res("bcs")
        R_EL, R_TOT, R_PFX, R_CUMP = Res("EL"), Res("TOT"), Res("PFX"), Res("CUMP")

        R_wus, R_wos, R_wgs, R_wps, R_wds = (Res("wus"), Res("wos"), Res("wgs"),
                                              Res("wps"), Res("wds"))

        conv_list = [
            lambda: (T.dma("pool", wu_s, w_in[:, 1544:2056], writes=[R_wus]),
                     T.dma("pool", wo_s, w_out, writes=[R_wos])),
            lambda: T.dma("pool", wg_s.rearrange("r (a c) -> r a c", a=2),
                          w_gate.rearrange("r (a c) -> r a c", a=2), writes=[R_wgs]),
            lambda: T.dma("pool", wp_s.rearrange("r (a c) -> r a c", a=2),
                          w_up.rearrange("r (a c) -> r a c", a=2), writes=[R_wps]),
            lambda: T.dma("pool", wd_s, w_down, writes=[R_wds]),
        ]

        T.barrier()

        def norm_T(srcs, dst, dst_res, wvec, pbank, mode, evac_engs, part="all"):
            n = len(srcs)
            P = srcs[0][2]
            if part in ("all", "elem"):
                norm_elem(srcs, n, P, mode)
            if part in ("all", "tr"):
                norm_tr(n, P, dst, dst_res, wvec, pbank, evac_engs)

        def norm_elem(srcs, n, P, mode, part="all"):
            if part in ("all", "a"):
                norm_elem_a(srcs, n, P, mode)
            if part in ("all", "b"):
                norm_elem_b(srcs, n, P, mode)

        def norm_elem_a(srcs, n, P, mode):
            for i, (ap, res, _) in enumerate(srcs):
                if mode == "act":
                    T.op("act", lambda: nc.scalar.activation(
                        out=junk[i % 2][0:P, :], in_=ap, func=AF.Square, accum_out=ssq[0:P, i:i + 1]),
                        reads=[res], writes=[R_junk[i % 2]] + ([R_ssq] if i in (0, n - 1) else []))
                else:
                    T.op("dve", lambda: nc.vector.scalar_tensor_tensor(
                        out=junk[i % 2][0:P, :], in0=ap, scalar=1.0, in1=ap, op0=ALU.mult, op1=ALU.mult,
                        accum_out=ssq[0:P, i:i + 1]), reads=[res],
                        writes=[R_junk[i % 2]] + ([R_ssq] if i in (0, n - 1) else []))

        def norm_elem_b(srcs, n, P, mode):
            T.op("act", lambda: nc.scalar.activation(
                out=rs[0:P, 0:n], in_=ssq[0:P, 0:n], func=AF.Sqrt, bias=EPS, scale=1.0 / D),
                reads=[R_ssq], writes=[R_rs])
            T.op("dve", lambda: nc.vector.reciprocal(out=rs[0:P, 0:n], in_=rs[0:P, 0:n]),
                 reads=[R_rs], writes=[R_rs])
            for i, (ap, res, _) in enumerate(srcs):
                if mode == "act" and i % 2 == 0:
                    T.op("act", lambda: nc.scalar.activation(
                        out=xsel["xn"][0:P, i, :], in_=ap, func=AF.Copy, scale=rs[0:P, i:i + 1]),
                        reads=[res, R_rs], writes=[xsel["R"][i]])
                else:
                    T.op("dve", lambda: nc.vector.tensor_scalar(
                        out=xsel["xn"][0:P, i, :], in0=ap, scalar1=rs[0:P, i:i + 1], scalar2=None,
                        op0=ALU.mult), reads=[res, R_rs], writes=[xsel["R"][i]])

        def norm_tr(n, P, dst, dst_res, wvec, pbank, evac_engs, kcs=range(8)):
            ncol = n * P
            xcur, rxcur = xsel["xn"], xsel["R"]
            for kc in kcs:
                bk = pbank[kc % len(pbank)]
                ptv = PBb[bk]
                for i in range(n):
                    T.op("pe", lambda: nc.tensor.transpose(
                        out=ptv[:, i * P:(i + 1) * P], in_=xcur[0:P, i, kc * 128:(kc + 1) * 128],
                        identity=ident[0:P, 0:P]),
                        reads=[rxcur[i]], writes=[R_PB[bk]], sig=(i == n - 1))
                ee = evac_engs[kc % len(evac_engs)]
                if ee == "act":
                    T.op("act", lambda: nc.scalar.activation(
                        out=dst[:, kc, 0:ncol], in_=ptv[:, 0:ncol], func=AF.Copy,
                        scale=wvec[:, kc:kc + 1]), reads=[R_PB[bk]], writes=[dst_res[kc]])
                else:
                    T.op("dve", lambda: nc.vector.tensor_scalar(
                        out=dst[:, kc, 0:ncol], in0=ptv[:, 0:ncol], scalar1=wvec[:, kc:kc + 1],
                        scalar2=None, op0=ALU.mult), reads=[R_PB[bk]], writes=[dst_res[kc]])

        def proj_qknorm(W, R_W, p, src, src_res, ncol, wv2, out_ap, out_res, pk, pss, mode,
                        part="all"):
            if part in ("all", "mm"):
                proj_mm(W, R_W, p, src, src_res, ncol, pk)
            if part in ("all", "norm"):
                proj_norm(p, ncol, wv2, out_ap, out_res, pk, pss)

        def proj_mm(W, R_W, p, src, src_res, ncol, pk, kcs=range(8)):
            for kc in kcs:
                T.op("pe", lambda: nc.tensor.matmul(
                    PB[pk][:, 0:ncol], lhsT=W[:, kc, p * 128:(p + 1) * 128], rhs=src[:, kc, 0:ncol],
                    start=(kc == 0), stop=(kc == 7)),
                    reads=[R_W, src_res[kc]], writes=[R_PB[pk]], sig=(kc == 7))
            if 7 not in kcs:
                return
            T.op("act", lambda: nc.scalar.activation(
                out=sqb[p][:, 0:ncol], in_=PB[pk][:, 0:ncol], func=AF.Square),
                reads=[R_PB[pk]], writes=[R_sq[p]])

        def proj_norm(p, ncol, wv2, out_ap, out_res, pk, pss):
            T.op("pe", lambda: nc.tensor.matmul(
                PB[pss][:, 0:ncol], lhsT=bones[:], rhs=sqb[p][:, 0:ncol], start=True, stop=True),
                reads=[R_sq[p]], writes=[R_PB[pss]])
            T.op("act", lambda: nc.scalar.activation(
                out=rstd[p][:, 0:ncol], in_=PB[pss][:, 0:ncol], func=AF.Sqrt, bias=EPS,
                scale=1.0 / 64), reads=[R_PB[pss]], writes=[R_rstd[p]])
            T.op("dve", lambda: nc.vector.reciprocal(
                out=rstd[p][:, 0:ncol], in_=rstd[p][:, 0:ncol]),
                reads=[R_rstd[p]], writes=[R_rstd[p]])
            for (q0, q1, oap) in out_ap:
                T.op("dve", lambda: nc.vector.scalar_tensor_tensor(
                    out=oap, in0=PB[pk][q0:q1, 0:ncol], scalar=wv2[q0:q1, 0:1],
                    in1=rstd[p][q0:q1, 0:ncol], op0=ALU.mult, op1=ALU.mult),
                    reads=[R_PB[pk], R_rstd[p]], writes=out_res)

        xdma_state = {"next": 0}

        def issue_x(tile_rows, upto):
            while xdma_state["next"] < min(upto, len(tile_rows)):
                k = xdma_state["next"]
                r0 = tile_rows[k]
                T.dma("sp", xs[k % XR][:], xv[r0:r0 + 128, :], writes=[R_xs[k % XR]])
                xdma_state["next"] += 1

        for hp in range(2):
            kcol = 512 + hp * 256
            T.dma("pool", Wk[:], w_in[:, kcol:kcol + 256].rearrange("(k p) c -> p k c", p=128),
                  writes=[R_Wk])
            T.dma("pool", Wvf[:, :, 0:256],
                  w_in[:, 1024 + hp * 256:1024 + hp * 256 + 256].rearrange("(k p) c -> p k c", p=128),
                  writes=[R_Wvf])
            T.dma("pool", Wvf[:, :, 256:260],
                  w_in[:, 1536 + hp * 4:1536 + hp * 4 + 4].rearrange("(k p) c -> p k c", p=128),
                  writes=[R_Wvf])
            T.dma("pool", Wq[:], w_in[:, hp * 256:hp * 256 + 256].rearrange("(k p) c -> p k c", p=128),
                  writes=[R_Wq])
            for hh in range(4):
                T.op("pool", lambda: nc.gpsimd.tensor_copy(out=VA[:, :, hh, 64], in_=flagS[:, :]))
            if hp == 0:
                T.op("pool", lambda: nc.gpsimd.memset(VAf[:, NV * 260:NV * 260 + 64], 0.0))
                for qb in range(2):
                    T.op("pool", lambda: nc.gpsimd.memset(QT[qb][:], 0.0), writes=[R_QT[qb]])

            rows1 = [t * 128 for t in range(NV)]
            xdma_state["next"] = 0
            base_k = 0
            def p1_info(g):
                tiles = list(range(4 * g, min(4 * g + 4, NV)))
                srcs = [(xs[t % XR][:], R_xs[t % XR], 128) for t in tiles]
                return tiles, srcs

            def p1_sel(g):
                xsel["xn"], xsel["R"] = (xn, R_xn) if g % 2 == 0 else (xnB, R_xnB)

            def p1_elem(g):
                tiles, srcs = p1_info(g)
                issue_x(rows1, 4 * g + XR)
                p1_sel(g)
                norm_elem(srcs, len(srcs), 128, "act")

            def p1_tr_items(g):
                tiles, srcs = p1_info(g)

                def mk(kc):
                    def f():
                        p1_sel(g)
                        norm_tr(len(tiles), 128, nT[g % 2], R_nT[g % 2], n1w, [0, 1], ["dve", "act"],
                                kcs=[kc])
                    return f
                return [mk(kc) for kc in range(8)]

            def p1_kv_items(g):
                tiles, _ = p1_info(g)
                ncol = len(tiles) * 128
                b = g % 2

                def kmm(p):
                    return lambda: proj_mm(Wk, R_Wk, p, nT[b], R_nT[b], ncol, 2 + p)

                def vmm(i, t):
                    def f():
                        bk = 5 + (i % 2)
                        for kc in range(8):
                            T.op("pe", lambda: nc.tensor.matmul(
                                PB[bk][:, 0:260], lhsT=nT[b][:, kc, i * 128:(i + 1) * 128],
                                rhs=Wvf[:, kc, :], start=(kc == 0), stop=(kc == 7)),
                                reads=[R_Wvf, R_nT[b][kc]], writes=[R_PB[bk]], sig=(kc == 7))
                        T.op("dve", lambda: nc.vector.tensor_copy(
                            out=VA[:, t, :, 0:64],
                            in_=PB[bk][:, 0:256].rearrange("p (a b) -> p a b", a=4)),
                            reads=[R_PB[bk]])
                        T.op("dve", lambda: nc.vector.tensor_tensor(
                            out=FG[:, t * 4:(t + 1) * 4], in0=PB[bk][:, 256:260],
                            in1=bfg[:, hp * 4:(hp + 1) * 4], op=ALU.add), reads=[R_PB[bk]])
                    return f
                items = [kmm(0)]
                vi = [vmm(i, t) for i, t in enumerate(tiles)]
                items += vi[0:1] + [kmm(1)] + vi[1:]
                return items

            def p1_tail(g):
                tiles, _ = p1_info(g)
                ncol = len(tiles) * 128
                for p in range(2):
                    proj_norm(p, ncol, kw2,
                              [(0, 128, KT[:, p, tiles[0] * 128:tiles[0] * 128 + ncol])], [],
                              2 + p, (4, 7)[p])

            p1_elem(0)
            if NG > 1:
                p1_elem(1)
            for f in p1_tr_items(0):
                f()
            for g in range(NG):
                if g + 2 < NG:
                    p1_elem(g + 2)
                A = p1_tr_items(g + 1) if g + 1 < NG else []
                Bk = p1_kv_items(g)
                for k in range(max(len(A), len(Bk))):
                    if k < len(A):
                        A[k]()
                    if k < len(Bk):
                        Bk[k]()
                p1_tail(g)
            xsel["xn"], xsel["R"] = xn, R_xn
            T.barrier()
            stop_here("p1_%d" % hp)
            NC4 = NV * 4
            T.op("act", lambda: nc.scalar.activation(out=EL[:], in_=FG[:], func=AF.Exp, scale=-1.0),
                 writes=[R_EL])
            T.op("act", lambda: nc.scalar.activation(out=EL[:], in_=EL[:], func=AF.Ln, bias=1.0),
                 reads=[R_EL], writes=[R_EL])
            T.op("pe", lambda: nc.tensor.matmul(PB[2][:, 0:NC4], lhsT=trif[:], rhs=EL[:],
                                                start=True, stop=True),
                 reads=[R_EL], writes=[R_PB[2]])
            T.op("pe", lambda: nc.tensor.matmul(PB[3][:, 0:NC4], lhsT=onesf[:], rhs=EL[:],
                                                start=True, stop=True),
                 reads=[R_EL], writes=[R_PB[3]])
            T.op("dve", lambda: nc.vector.tensor_copy(out=TOT[:], in_=PB[3][:, 0:NC4]),
                 reads=[R_PB[3]], writes=[R_TOT])
            T.op("dve", lambda: nc.vector.memset(PFX[:, 0:4], 0.0), writes=[R_PFX])
            for t in range(NV):
                T.op("dve", lambda: nc.vector.tensor_tensor(
                    out=PFX[:, (t + 1) * 4:(t + 2) * 4], in0=PFX[:, t * 4:(t + 1) * 4],
                    in1=TOT[:, t * 4:(t + 1) * 4], op=ALU.add),
                    reads=[R_TOT, R_PFX], writes=[R_PFX])
            T.op("dve", lambda: nc.vector.tensor_tensor(
                out=CUMP[:], in0=PB[2][:, 0:NC4], in1=PFX[:, 0:NC4], op=ALU.add),
                reads=[R_PB[2], R_PFX], writes=[R_CUMP])
            T.barrier()
            stop_here("cum_%d" % hp)

            rows2 = [(2 * m + 2) * 128 for m in range(NOWN)]
            xdma_state["next"] = 0

            def q_sel(j):
                xsel["xn"], xsel["R"] = (xn, R_xn) if j % 2 == 0 else (xnB, R_xnB)

            def q_srcs(j):
                return [(xs[(4 * j + i) % XR][:], R_xs[(4 * j + i) % XR], 128) for i in range(4)]

            def q_elem(j):
                issue_x(rows2, 4 * j + XR)
                q_sel(j)
                norm_elem(q_srcs(j), 4, 128, "dve", part="a")

            def q_items(j):
                b = j % 2
                items = [None] * 6

                def eb():
                    q_sel(j)
                    norm_elem(q_srcs(j), 4, 128, "dve", part="b")
                items.append(eb)
                items += [None] * 3

                def trk(kc):
                    def f():
                        q_sel(j)
                        norm_tr(4, 128, nT[b], R_nT[b], n1w, [6], ["dve"], kcs=[kc])
                    return f
                items += [trk(kc) for kc in range(8)]
                pks = (7, 6)
                for p in range(2):
                    for hh in range(2):
                        items.append(lambda p=p, hh=hh: proj_mm(
                            Wq, R_Wq, p, nT[b], R_nT[b], 512, pks[p], kcs=range(4 * hh, 4 * hh + 4)))
                for p in range(2):
                    items.append(lambda p=p: proj_norm(
                        p, 512, qw2,
                        [(0, 64, QT[b][0:64, 2 * p, :]), (64, 128, QT[b][64:128, 2 * p + 1, :])],
                        [R_QT[b]], pks[p], 5))
                return items

            steps = []
            for j in range(nslot):
                for hl in range(4):
                    nfull = 8 * j + 2
                    seq = [(kt, 0, None) for kt in range(nfull)]
                    for k in range(7):
                        i0 = (k + 1) // 2
                        seq.append((nfull + k, i0 * 128, (i0 if k % 2 == 0 else None)))
                    for n_, (kt, c0, dg) in enumerate(seq):
                        steps.append(dict(j=j, hl=hl, kt=kt, c0=c0, dg=dg, first=(n_ == 0),
                                          last=(n_ == len(seq) - 1)))
            LA = 2
            q_elem(0)
            for it_ in q_items(0):
                if it_ is not None:
                    it_()
            qpend = []
            state = {"hidx": 0}

            def emit_qk(si, s):
                j, hl, kt, c0 = s["j"], s["hl"], s["kt"], s["c0"]
                p, half = hl // 2, hl % 2
                r0 = half * 64
                b = j % 2
                bk = si % 3
                if s["first"]:
                    hb = (j * 4 + hl) % 2
                    ntile = 8 * j + 9
                    T.op("dve", lambda: nc.vector.tensor_scalar(
                        out=BIASJ[hb][:, 0:ntile],
                        in0=CUMP[:, 0:ntile * 4].rearrange("p (t h) -> p t h", h=4)[:, :, hl],
                        scalar1=PFX[:, (8 * j + 6) * 4 + hl:(8 * j + 6) * 4 + hl + 1], scalar2=None,
                        op0=ALU.subtract), writes=[R_BJ[hb]])
                T.op("pe", lambda: nc.tensor.matmul(
                    PB[bk][:, c0:512], lhsT=KT[:, p, kt * 128:(kt + 1) * 128],
                    rhs=QT[b][:, hl, c0:512], start=True, stop=True),
                    reads=[R_QT[b]], writes=[R_PB[bk]])
                hb = (j * 4 + hl) % 2
                ps = si % NPT
                T.op("act", lambda: nc.scalar.activation(
                    out=PT[ps][:, c0:512], in_=PB[bk][:, c0:512], func=AF.Exp,
                    bias=BIASJ[hb][:, kt:kt + 1], scale=0.125),
                    reads=[R_PB[bk], R_BJ[hb]], writes=[R_PT[ps]])
                if s["dg"] is not None:
                    i0 = s["dg"]
                    T.op("dve", lambda: nc.vector.tensor_tensor(
                        out=PT[ps][:, i0 * 128:(i0 + 1) * 128], in0=PT[ps][:, i0 * 128:(i0 + 1) * 128],
                        in1=tri[:], op=ALU.mult), reads=[R_PT[ps]], writes=[R_PT[ps]])

            def emit_pv(si, s):
                j, hl, kt, c0 = s["j"], s["hl"], s["kt"], s["c0"]
                ps = si % NPT
                hidx = j * 4 + hl
                ob = 3 + (hidx % 2)
                T.op("pe", lambda: nc.tensor.matmul(
                    PB[ob][:, c0:512],
                    lhsT=VAf[:, kt * 260 + hl * 65:kt * 260 + hl * 65 + 128], rhs=PT[ps][:, c0:512],
                    start=s["first"], stop=s["last"]),
                    reads=[R_PT[ps]], writes=[R_PB[ob]], sig=True)
                if s["last"]:
                    rsr = 64
                    o0 = 0
                    T.op("dve", lambda: nc.vector.reciprocal(
                        out=rec[rsr:rsr + 1, :], in_=PB[ob][rsr:rsr + 1, :]),
                        reads=[R_PB[ob]], writes=[R_rec])
                    T.op("pe", lambda: nc.tensor.matmul(
                        PB[5][o0:o0 + 64, :], lhsT=onesf[rsr:rsr + 1, 0:64], rhs=rec[rsr:rsr + 1, :],
                        start=True, stop=True), reads=[R_rec], writes=[R_PB[5]])
                    T.op("dve", lambda: nc.vector.tensor_copy(
                        out=bcs[o0:o0 + 64, :], in_=PB[5][o0:o0 + 64, :]),
                        reads=[R_PB[5]], writes=[R_bcs])
                    if hp == 0:
                        T.op("dve", lambda: nc.vector.tensor_tensor(
                            out=AT[0:64, hl, j * 512:(j + 1) * 512], in0=PB[ob][0:64, :],
                            in1=bcs[0:64, :], op=ALU.mult),
                            reads=[R_PB[ob], R_bcs])
                    else:
                        ab = hidx % 2
                        T.op("dve", lambda: nc.vector.tensor_tensor(
                            out=atmp[ab][:, :], in0=PB[ob][0:64, :],
                            in1=bcs[0:64, :], op=ALU.mult),
                            reads=[R_PB[ob], R_bcs], writes=[R_atmp[ab]])
                        T.dma("sp", AT[64:128, hl, j * 512:(j + 1) * 512], atmp[ab][:, :],
                              reads=[R_atmp[ab]])

            ns = len(steps)
            for si in range(ns + LA):
                if si < ns:
                    s_ = steps[si]
                    if s_["first"] and s_["hl"] == 0:
                        while qpend:
                            it_ = qpend.pop(0)
                            if it_ is not None:
                                it_()
                        if s_["j"] + 1 < nslot:
                            q_elem(s_["j"] + 1)
                            qpend.extend(q_items(s_["j"] + 1))
                        if hp == 0 and s_["j"] >= 2 and conv_list:
                            conv_list.pop(0)()
                    emit_qk(si, s_)
                if si >= LA:
                    emit_pv(si - LA, steps[si - LA])
                if qpend:
                    it_ = qpend.pop(0)
                    if it_ is not None:
                        it_()
            while qpend:
                it_ = qpend.pop(0)
                if it_ is not None:
                    it_()
            while hp == 0 and conv_list:
                conv_list.pop(0)()
            xsel["xn"], xsel["R"] = xn, R_xn
            T.barrier()
            stop_here("p2_%d" % hp)

        st_att.close()

        st3 = ExitStack()
        H = [sb("H%d" % i, [128, D], F32, st3) for i in range(4)]
        XH = sb("XH", [64, D], F32, st3)
        xsF = [sb("xsF%d" % i, [128, D], F32, st3) for i in range(2)]
        HX = sb("HX", [128, NF * 512], BF16, st3)
        xnF = sb("xnF", [128, 4, D], BF16, st3)
        xnh = sb("xnh", [64, D], BF16, st3)
        nT3 = sb("nT3", [128, 8, 512], BF16, st3)
        nTa = sb("nTa", [128, 8, 512], BF16, st3)
        nTh = sb("nTh", [128, 8, 64], BF16, st3)
        U = sb("U", [128, 4, 144], F32, st3)
        UA = sb("UA", [128, 4, 144], F32, st3)
        UB = sb("UB", [128, 4, 144], F32, st3)
        PL = sb("PL", [128, 512], BF16, st3)
        PO = sb("PO", [128, 4, 512], BF16, st3)
        Wd = sb("Wd", [128, NF, D], BF16, st3)
        WuR = sb("WuR", [128, 8, 512], BF16, st3)
        NST = 3
        RING = [sb("ring%d" % i, [128, 4096], BF16, st3) for i in range(NST)]
        Wp = sb("Wp", [128, 4, 128], BF16, st3)
        sg = [sb("sg%d" % i, [128, 512], BF16, st3) for i in range(2)]
        ssq3 = sb("ssq3", [128, 4], F32, st3)
        rs3 = sb("rs3", [128, 4], F32, st3)
        ssqh = sb("ssqh", [64, 1], F32, st3)
        rsh = sb("rsh", [64, 1], F32, st3)

        R_H = [Res("H%d" % i) for i in range(4)]
        R_XH = Res("XH")
        R_xsF = [Res("xsF0"), Res("xsF1")]
        R_HX = [Res("HX%d" % f) for f in range(NF)]
        R_xnF = [Res("xnF%d" % i) for i in range(4)]
        R_xnh = Res("xnh")
        R_nT3 = [Res("nT3_%d" % k) for k in range(8)]
        R_nTa = [Res("nTa_%d" % k) for k in range(8)]
        R_nTh = [Res("nTh_%d" % k) for k in range(8)]
        R_U, R_UA, R_UB = Res("U"), Res("UA"), Res("UB")
        R_PL = Res("PL")
        R_PO = [Res("PO%d" % g) for g in range(4)]
        R_Wd, R_Wp, R_WuR = Res("Wd"), Res("Wp"), Res("WuR")
        R_RING = [Res("ring%d" % i) for i in range(NST)]
        R_sg = [Res("sg0"), Res("sg1")]
        R_ssq3, R_rs3, R_ssqh, R_rsh = Res("ssq3"), Res("rs3"), Res("ssqh"), Res("rsh")

        T.dma("sp", WuR[:], wu_s.rearrange("(k p) c -> p k c", p=128), reads=[R_wus], writes=[R_WuR])
        T.dma("sp", Wd[:], wd_s.rearrange("(f p) c -> p f c", p=128), reads=[R_wds], writes=[R_Wd])
        T.dma("pool", Wp[:], w_pool.rearrange("g c d -> c g d"), writes=[R_Wp])

        NGU = DFF // 256
        chunk_list = []
        for j in range(nslot):
            chunk_list.append(("woa", j))
            chunk_list.append(("wop", j))
            for c in range(NGU):
                chunk_list.append(("gu", c))
        ws = {"next": 0}

        def issue_chunk(k):
            kind, arg = chunk_list[k]
            st = k % NST
            rg, rr = RING[st], R_RING[st]
            if kind == "woa":
                T.dma("sp", rg[0:64, 0:4096].rearrange("p (k c) -> p k c", k=4),
                      wo_s[0:256, :].rearrange("(k p) c -> p k c", p=64), reads=[R_wos], writes=[rr])
                rr.w = None
                T.dma("sp", rg[64:128, 0:4096].rearrange("p (k c) -> p k c", k=4),
                      wo_s[256:512, :].rearrange("(k p) c -> p k c", p=64), reads=[R_wos], writes=[rr])
            elif kind == "wop":
                T.dma("sp", rg[:, 0:4096].rearrange("p (k c) -> p k c", k=4),
                      wo_s[512:1024, :].rearrange("(k p) c -> p k c", p=128), reads=[R_wos], writes=[rr])
            else:
                c = arg
                T.dma("sp", rg[:, 0:2048].rearrange("p (k c) -> p k c", k=8),
                      wg_s[:, c * 256:(c + 1) * 256].rearrange("(k p) c -> p k c", p=128),
                      reads=[R_wgs], writes=[rr])
                rr.w = None
                T.dma("sp", rg[:, 2048:4096].rearrange("p (k c) -> p k c", k=8),
                      wp_s[:, c * 256:(c + 1) * 256].rearrange("(k p) c -> p k c", p=128),
                      reads=[R_wps], writes=[rr])

        def stream_upto(k):
            while ws["next"] <= min(k, len(chunk_list) - 1):
                issue_chunk(ws["next"])
                ws["next"] += 1

        ck = {"i": 0}

        def next_chunk():
            k = ck["i"]
            ck["i"] += 1
            stream_upto(k + NST - 2)
            return RING[k % NST], R_RING[k % NST]

        def sq_rs(srcs, P, ssq_t, rs_t, R_sq_, R_rs_, xn_of, xn_res_of):
            n = len(srcs)
            for i, (ap, res) in enumerate(srcs):
                T.op("act", lambda: nc.scalar.activation(
                    out=xn_of(i), in_=ap, func=AF.Square, accum_out=ssq_t[0:P, i:i + 1]),
                    reads=[res], writes=xn_res_of(i) + ([R_sq_] if i in (0, n - 1) else []))
            T.op("act", lambda: nc.scalar.activation(
                out=rs_t[0:P, 0:n], in_=ssq_t[0:P, 0:n], func=AF.Sqrt, bias=EPS, scale=1.0 / D),
                reads=[R_sq_], writes=[R_rs_])
            T.op("dve", lambda: nc.vector.reciprocal(out=rs_t[0:P, 0:n], in_=rs_t[0:P, 0:n]),
                 reads=[R_rs_], writes=[R_rs_])
            for i, (ap, res) in enumerate(srcs):
                if i % 2 == 0:
                    T.op("act", lambda: nc.scalar.activation(
                        out=xn_of(i), in_=ap, func=AF.Copy, scale=rs_t[0:P, i:i + 1]),
                        reads=[res, R_rs_], writes=xn_res_of(i))
                else:
                    T.op("dve", lambda: nc.vector.tensor_scalar(
                        out=xn_of(i), in0=ap, scalar1=rs_t[0:P, i:i + 1], scalar2=None, op0=ALU.mult),
                        reads=[res, R_rs_], writes=xn_res_of(i))

        def tr3(n, P, xn_of, xn_res_of, dst, dst_res, wvec, kcs, pbank=(0, 1)):
            ncol = n * P
            for kc in kcs:
                bk = pbank[kc % 2]
                ptv = PBb[bk]
                for i in range(n):
                    T.op("pe", lambda: nc.tensor.transpose(
                        out=ptv[:, i * P:(i + 1) * P], in_=xn_of(i)[:, kc * 128:(kc + 1) * 128],
                        identity=ident[0:P, 0:P]),
                        reads=xn_res_of(i), writes=[R_PB[bk]], sig=(i == n - 1))
                if kc % 2 == 0:
                    T.op("dve", lambda: nc.vector.tensor_scalar(
                        out=dst[:, kc, 0:ncol], in0=ptv[:, 0:ncol], scalar1=wvec[:, kc:kc + 1],
                        scalar2=None, op0=ALU.mult), reads=[R_PB[bk]], writes=[dst_res[kc]])
                else:
                    T.op("act", lambda: nc.scalar.activation(
                        out=dst[:, kc, 0:ncol], in_=ptv[:, 0:ncol], func=AF.Copy,
                        scale=wvec[:, kc:kc + 1]), reads=[R_PB[bk]], writes=[dst_res[kc]])

        def xn3(i):
            return HX[:, i * 1024:(i + 1) * 1024]

        def xn3_res(i):
            return [R_HX[2 * i], R_HX[2 * i + 1]]

        def front_elem(j):
            for i in range(4):
                r0 = (2 * (4 * j + i) + 2) * 128
                T.dma("sp", XH[i * 16:(i + 1) * 16, :], xv[r0 - 16:r0, :], writes=[R_XH])
                if i < 3:
                    R_XH.w = None
            n = 4
            for i in range(n):
                r0 = (2 * (4 * j + i) + 2) * 128
                T.dma("sp", xsF[i % 2][:], xv[r0:r0 + 128, :], writes=[R_xsF[i % 2]])
                T.op("act", lambda: nc.scalar.activation(
                    out=xnF[:, i, :], in_=xsF[i % 2][:], func=AF.Square, accum_out=ssq3[:, i:i + 1]),
                    reads=[R_xsF[i % 2]], writes=[R_xnF[i]] + ([R_ssq3] if i in (0, n - 1) else []))
                if i >= 0:
                    pass
            T.op("act", lambda: nc.scalar.activation(
                out=rs3[:, 0:n], in_=ssq3[:, 0:n], func=AF.Sqrt, bias=EPS, scale=1.0 / D),
                reads=[R_ssq3], writes=[R_rs3])
            T.op("dve", lambda: nc.vector.reciprocal(out=rs3[:, 0:n], in_=rs3[:, 0:n]),
                 reads=[R_rs3], writes=[R_rs3])
            for i in range(n):
                r0 = (2 * (4 * j + i) + 2) * 128
                T.dma("sp", xsF[i % 2][:], xv[r0:r0 + 128, :], writes=[R_xsF[i % 2]])
                if i % 2 == 0:
                    T.op("act", lambda: nc.scalar.activation(
                        out=xnF[:, i, :], in_=xsF[i % 2][:], func=AF.Copy, scale=rs3[:, i:i + 1]),
                        reads=[R_xsF[i % 2], R_rs3], writes=[R_xnF[i]])
                else:
                    T.op("dve", lambda: nc.vector.tensor_scalar(
                        out=xnF[:, i, :], in0=xsF[i % 2][:], scalar1=rs3[:, i:i + 1], scalar2=None,
                        op0=ALU.mult), reads=[R_xsF[i % 2], R_rs3], writes=[R_xnF[i]])
            sq_rs([(XH[:], R_XH)], 64, ssqh, rsh, R_ssqh, R_rsh, lambda i: xnh[:], lambda i: [R_xnh])

        def front_items(j):
            items = []
            for kc in range(8):
                def ftr(kc=kc):
                    tr3(4, 128, lambda i: xnF[:, i, :], lambda i: [R_xnF[i]], nTa, R_nTa, n1w, [kc])
                    tr3(1, 64, lambda i: xnh[:], lambda i: [R_xnh], nTh, R_nTh, n1w, [kc])
                items.append(ftr)

            def uproj(g):
                def f():
                    for kc in range(8):
                        T.op("pe", lambda: nc.tensor.matmul(
                            PB[4][:, :], lhsT=WuR[:, kc, g * 128:(g + 1) * 128], rhs=nTa[:, kc, :],
                            start=(kc == 0), stop=(kc == 7)),
                            reads=[R_WuR, R_nTa[kc]], writes=[R_PB[4]], sig=(kc == 7))
                    for kc in range(8):
                        T.op("pe", lambda: nc.tensor.matmul(
                            PB[5][:, 0:64], lhsT=WuR[:, kc, g * 128:(g + 1) * 128], rhs=nTh[:, kc, :],
                            start=(kc == 0), stop=(kc == 7)),
                            reads=[R_WuR, R_nTh[kc]], writes=[R_PB[5]], sig=(kc == 7))
                    w = 2 << g
                    T.op("dve", lambda: nc.vector.tensor_copy(
                        out=U[:, :, 16:144], in_=PB[4][:, :].rearrange("p (a b) -> p a b", a=4)),
                        reads=[R_PB[4]], writes=[R_U])
                    T.op("dve", lambda: nc.vector.tensor_copy(
                        out=U[:, :, 0:16], in_=PB[5][:, 0:64].rearrange("p (a b) -> p a b", a=4)),
                        reads=[R_PB[5], R_U], writes=[R_U])
                    cur, rcur = U, R_U
                    sh, lo, bi = 1, 0, 0
                    bufs = [(UA, R_UA), (UB, R_UB)]
                    while sh < w:
                        nxt, rnxt = bufs[bi]
                        bi ^= 1
                        T.op("dve", lambda: nc.vector.tensor_tensor(
                            out=nxt[:, :, lo + sh:144], in0=cur[:, :, lo + sh:144],
                            in1=cur[:, :, lo:144 - sh], op=ALU.add), reads=[rcur], writes=[rnxt])
                        cur, rcur = nxt, rnxt
                        lo += sh
                        sh *= 2
                    T.op("dve", lambda: nc.vector.scalar_tensor_tensor(
                        out=PL[:, :].rearrange("p (a b) -> p a b", a=4), in0=cur[:, :, 16:144],
                        scalar=1.0 / w, in1=U[:, :, 16:144], op0=ALU.mult, op1=ALU.subtract),
                        reads=[rcur, R_U], writes=[R_PL])
                return f

            def wpool(g):
                def f():
                    T.op("pe", lambda: nc.tensor.matmul(
                        PB[5][:, :], lhsT=Wp[:, g, :], rhs=PL[:, :], start=True, stop=True),
                        reads=[R_Wp, R_PL], writes=[R_PB[5]])
                    T.op("dve", lambda: nc.vector.tensor_scalar(
                        out=PO[:, g, :], in0=PB[5][:, :], scalar1=psc[:, g:g + 1], scalar2=None,
                        op0=ALU.mult), reads=[R_PB[5]], writes=[R_PO[g]])
                return f
            items.append(uproj(0))
            items.append(None)
            for g in range(1, 4):
                items.append(lambda g=g: (wpool(g - 1)(), uproj(g)()))
                items.append(None)
            items.append(wpool(3))
            return items

        front_elem(0)
        for it_ in front_items(0):
            if it_ is not None:
                it_()

        def load_res_tile(j, i):
            r0 = (2 * (4 * j + i) + 2) * 128
            T.dma("sp", xsF[i % 2][:], xv[r0:r0 + 128, :], writes=[R_xsF[i % 2]])

        load_res_tile(0, 0)
        load_res_tile(0, 1)
        for j in range(nslot):
            rga, rra = next_chunk()
            rgp, rrp = next_chunk()
            WoA = rga[:, 0:4096].rearrange("p (k c) -> p k c", k=4)
            WoP = rgp[:, 0:4096].rearrange("p (k c) -> p k c", k=4)
            for i in range(4):
                tok0 = (4 * j + i) * 128
                for nh in range(2):
                    bk = 6 + ((2 * i + nh) % 2)
                    for c in range(4):
                        T.op("pe", lambda: nc.tensor.matmul(
                            PB[bk][:, :], lhsT=AT[:, c, tok0:tok0 + 128],
                            rhs=WoA[:, c, nh * 512:(nh + 1) * 512], start=(c == 0), stop=False),
                            reads=[rra], writes=[R_PB[bk]], sig=False)
                    for g in range(4):
                        T.op("pe", lambda: nc.tensor.matmul(
                            PB[bk][:, :], lhsT=PO[:, g, i * 128:(i + 1) * 128],
                            rhs=WoP[:, g, nh * 512:(nh + 1) * 512], start=False, stop=(g == 3)),
                            reads=[rrp, R_PO[g]], writes=[R_PB[bk]], sig=(g == 3))
                    T.op("dve", lambda: nc.vector.tensor_tensor(
                        out=H[i][:, nh * 512:(nh + 1) * 512], in0=PB[bk][:, :],
                        in1=xsF[i % 2][:, nh * 512:(nh + 1) * 512], op=ALU.add),
                        reads=[R_PB[bk], R_xsF[i % 2]], writes=[R_H[i]])
                if i + 2 < 4:
                    load_res_tile(j, i + 2)
                T.op("act", lambda: nc.scalar.activation(
                    out=xn3(i), in_=H[i][:], func=AF.Square, accum_out=ssq3[:, i:i + 1]),
                    reads=[R_H[i]], writes=xn3_res(i) + ([R_ssq3] if i in (0, 3) else []))
            T.op("act", lambda: nc.scalar.activation(
                out=rs3[:, 0:4], in_=ssq3[:, 0:4], func=AF.Sqrt, bias=EPS, scale=1.0 / D),
                reads=[R_ssq3], writes=[R_rs3])
            T.op("dve", lambda: nc.vector.reciprocal(out=rs3[:, 0:4], in_=rs3[:, 0:4]),
                 reads=[R_rs3], writes=[R_rs3])
            for i in range(4):
                if i % 2 == 0:
                    T.op("act", lambda: nc.scalar.activation(
                        out=xn3(i), in_=H[i][:], func=AF.Copy, scale=rs3[:, i:i + 1]),
                        reads=[R_H[i], R_rs3], writes=xn3_res(i))
                else:
                    T.op("dve", lambda: nc.vector.tensor_scalar(
                        out=xn3(i), in0=H[i][:], scalar1=rs3[:, i:i + 1], scalar2=None, op0=ALU.mult),
                        reads=[R_H[i], R_rs3], writes=xn3_res(i))
            tr3(4, 128, xn3, xn3_res, nT3, R_nT3, n2w, range(8))
            fitems = []
            if j + 1 < nslot:
                front_elem(j + 1)
                fitems = front_items(j + 1)
            for c in range(NGU):
                rg, rr = next_chunk()
                Wg_c = rg[:, 0:2048].rearrange("p (k c) -> p k c", k=8)
                Wu_c = rg[:, 2048:4096].rearrange("p (k c) -> p k c", k=8)
                for ff in range(2):
                    f = 2 * c + ff
                    bg = 2 + (f % 2)
                    bu = 6 + (f % 2)
                    for kc in range(8):
                        T.op("pe", lambda: nc.tensor.matmul(
                            PB[bg][:, :], lhsT=Wg_c[:, kc, ff * 128:(ff + 1) * 128], rhs=nT3[:, kc, :],
                            start=(kc == 0), stop=(kc == 7)),
                            reads=[rr, R_nT3[kc]], writes=[R_PB[bg]], sig=(kc == 7))
                    for kc in range(8):
                        T.op("pe", lambda: nc.tensor.matmul(
                            PB[bu][:, :], lhsT=Wu_c[:, kc, ff * 128:(ff + 1) * 128], rhs=nT3[:, kc, :],
                            start=(kc == 0), stop=(kc == 7)),
                            reads=[rr, R_nT3[kc]], writes=[R_PB[bu]], sig=(kc == 7))
                    T.op("act", lambda: nc.scalar.activation(
                        out=sg[f % 2][:, :], in_=PB[bg][:, :], func=AF.Silu),
                        reads=[R_PB[bg]], writes=[R_sg[f % 2]])
                    T.op("dve", lambda: nc.vector.tensor_tensor(
                        out=HX[:, f * 512:(f + 1) * 512], in0=PB[bu][:, :], in1=sg[f % 2][:, :],
                        op=ALU.mult), reads=[R_PB[bu], R_sg[f % 2]], writes=[R_HX[f]])
                    if fitems:
                        it_ = fitems.pop(0)
                        if it_ is not None:
                            it_()
            while fitems:
                it_ = fitems.pop(0)
                if it_ is not None:
                    it_()
            if j + 1 < nslot:
                stream_upto(ck["i"] + 1)
                load_res_tile(j + 1, 0)
                load_res_tile(j + 1, 1)
            for i in range(4):
                for nh in range(2):
                    bk = 4 + ((2 * i + nh) % 2)
                    for f in range(NF):
                        T.op("pe", lambda: nc.tensor.matmul(
                            PB[bk][:, :], lhsT=HX[:, f * 512 + i * 128:f * 512 + (i + 1) * 128],
                            rhs=Wd[:, f, nh * 512:(nh + 1) * 512], start=(f == 0), stop=(f == NF - 1)),
                            reads=[R_Wd, R_HX[f]], writes=[R_PB[bk]], sig=(f == NF - 1))
                    T.op("dve", lambda: nc.vector.tensor_tensor(
                        out=H[i][:, nh * 512:(nh + 1) * 512], in0=PB[bk][:, :],
                        in1=H[i][:, nh * 512:(nh + 1) * 512], op=ALU.add),
                        reads=[R_PB[bk], R_H[i]], writes=[R_H[i]])
                m = 4 * j + i
                T.dma("sp", out_d[m * 128:(m + 1) * 128, :], H[i][:], reads=[R_H[i]])
        T.barrier()
        st3.close()
    return nc


def make_core_inputs(x_b, meta, role, nslot, shared):
    NV = 8 * nslot + 1
    lead = 2 if role == 0 else 1
    nx = NV - lead
    xvv = np.zeros((NV * 128, D), np.float32)
    xvv[lead * 128 - 16:lead * 128] = meta
    xvv[lead * 128:] = x_b[:nx * 128]
    fl = np.zeros((NV * 128,), np.float32)
    fl[lead * 128 - 16:] = 1.0
    d = dict(shared)
    d["xv"] = xvv
    d["flag"] = np.ascontiguousarray(fl.reshape(NV, 128).T)
    return d


def make_shared(norm1_w, w_in, b_fgate, q_norm_w, k_norm_w, w_pool, pool_scale, w_out,
                norm2_w, w_gate, w_up, w_down):
    f = lambda a: np.ascontiguousarray(np.asarray(a, np.float32))
    return dict(
        w_in=f(w_in[0]), w_out=f(w_out[0]), w_gate=f(w_gate[0]), w_up=f(w_up[0]),
        w_down=f(w_down[0]), w_pool=f(w_pool[0]),
        n1w=f(norm1_w[0].reshape(8, 128).T), n2w=f(norm2_w[0].reshape(8, 128).T),
        qw2=f(np.concatenate([q_norm_w[0], q_norm_w[0]]).reshape(128, 1)),
        kw2=f(np.concatenate([k_norm_w[0], k_norm_w[0]]).reshape(128, 1)),
        bfg=f(np.broadcast_to(b_fgate[0][None, :], (128, 8))),
        psc=f(pool_scale[0].reshape(4, 128).T),
    )


def run(x, meta_tokens, shared, nslot, core_ids, stop=None):
    nc = build(nslot, stop)
    B = x.shape[0]
    in_maps = []
    for c in range(2 * B):
        in_maps.append(make_core_inputs(np.asarray(x[c // 2], np.float32),
                                        np.asarray(meta_tokens, np.float32), c % 2, nslot, shared))
    res = run_bass_kernel_spmd(nc, in_maps, core_ids=core_ids)
    S = 8 * nslot * 128
    out = np.zeros((B, S, D), np.float32)
    for c in range(2 * B):
        o = np.asarray(res.results[c]["out"]).reshape(4 * nslot, 128, D)
        out[c // 2].reshape(8 * nslot, 128, D)[(c % 2)::2] = o
    return out


def kernel(x, meta_tokens, norm1_w, w_in, b_fgate, q_norm_w, k_norm_w, w_pool, pool_scale,
           w_out, norm2_w, w_gate, w_up, w_down):
    shared = make_shared(norm1_w, w_in, b_fgate, q_norm_w, k_norm_w, w_pool, pool_scale, w_out,
                         norm2_w, w_gate, w_up, w_down)
    return run(np.asarray(x), np.asarray(meta_tokens), shared, 8, list(range(8)))
```

```python
import numpy as np
from contextlib import ExitStack
import concourse.bass as bass
import concourse.mybir as mybir
from concourse.bass_utils import run_bass_kernel_spmd

F32 = mybir.dt.float32
BF16 = mybir.dt.bfloat16
AF = mybir.ActivationFunctionType
ALU = mybir.AluOpType

D = 1024
NH = 8
DFF = 2816
NF = DFF // 128
EPS = 1e-6
XR = 6


class Res:
    __slots__ = ("w", "r", "dsem", "dcnt", "name", "excl")

    def __init__(self, name="", excl=False):
        self.excl = excl
        self.w = None
        self.r = {}
        self.dsem = None
        self.dcnt = 0
        self.name = name


class Tracker:
    def __init__(self, nc, es):
        self.nc = nc
        self.es = es
        self.eng = dict(pe=nc.tensor, act=nc.scalar, dve=nc.vector, pool=nc.gpsimd, sp=nc.sync)
        self.sem = {k: es.enter_context(nc.semaphore("s_" + k)) for k in self.eng}
        self.cnt = {k: 0 for k in self.eng}
        self.seen = {k: {} for k in self.eng}
        self.pend = {k: ([], []) for k in self.eng}
        self.dres = []
        self.nd = 0

    def _wait(self, e, toks):
        for tk in toks:
            if tk is None:
                continue
            sem, val = tk
            if self.seen[e].get(sem.name, 0) >= val:
                continue
            self.eng[e].wait_ge(sem, val)
            self.seen[e][sem.name] = val

    def _deps(self, e, reads, writes):
        toks = []
        for e2, (pr, pw) in self.pend.items():
            if e2 == e:
                continue
            for r in reads:
                assert all(r is not x for x in pw), "read of pending write %s" % r.name
            for w in writes:
                assert all(w is not x for x in pw) and all(w is not x for x in pr), \
                    "write of pending %s" % w.name
        for r in reads:
            toks.append(r.w)
            if r.excl:
                own = self.sem[e].name
                toks.extend(tk for nm, tk in r.r.items() if nm != own)
        for w in writes:
            toks.append(w.w)
            toks.extend(w.r.values())
        return toks

    def op(self, e, fn, reads=(), writes=(), sig=True, extra=()):
        self._wait(e, self._deps(e, reads, writes) + list(extra))
        ins = fn()
        pr, pw = self.pend[e]
        pr.extend(reads)
        pw.extend(writes)
        if sig:
            self.cnt[e] += 1
            ins.then_inc(self.sem[e], 1)
            tk = (self.sem[e], self.cnt[e])
            for r in pr:
                r.r[self.sem[e].name] = tk
            for w in pw:
                w.w = tk
                w.r = {}
            self.pend[e] = ([], [])
            return tk
        return None

    def dma(self, q, out, in_, reads=(), writes=(), extra=()):
        self._wait(q, self._deps(q, reads, writes) + list(extra))
        res = writes[0] if writes else reads[0]
        if res.dsem is None:
            res.dsem = self.es.enter_context(self.nc.semaphore("d%d" % self.nd))
            self.nd += 1
            self.dres.append(res)
        res.dcnt += 16
        self.eng[q].dma_start(out=out, in_=in_).then_inc(res.dsem, 16)
        tk = (res.dsem, res.dcnt)
        for r in reads:
            r.r[res.dsem.name] = tk
        for w in writes:
            w.w = tk
            w.r = {}
        return tk

    def barrier(self):
        for e in self.eng:
            assert not self.pend[e][0] and not self.pend[e][1], "pending at barrier"
        toks = [(self.sem[e], self.cnt[e]) for e in self.eng if self.cnt[e] > 0]
        toks += [(r.dsem, r.dcnt) for r in self.dres]
        for e in self.eng:
            self._wait(e, toks)


class _Stop(Exception):
    pass


def build(nslot, stop=None):
    try:
        return _build(nslot, stop)
    except _Stop as e:
        return e.args[0]


def _build(nslot, stop=None):
    NV = 8 * nslot + 1
    NOWN = 4 * nslot
    NVC = NV * 128
    NG = (NV + 3) // 4
    nc = bass.Bass("TRN2", target_bir_lowering=False)

    def din(name, shape):
        return nc.dram_tensor(name, shape, F32, kind="ExternalInput").ap()

    xv = din("xv", [NVC, D])
    flag_d = din("flag", [128, NV])
    w_in = din("w_in", [D, 2056])
    w_out = din("w_out", [D, D])
    w_gate = din("w_gate", [D, DFF])
    w_up = din("w_up", [D, DFF])
    w_down = din("w_down", [DFF, D])
    w_pool = din("w_pool", [4, 128, 128])
    n1w_d = din("n1w", [128, 8])
    n2w_d = din("n2w", [128, 8])
    qw2_d = din("qw2", [128, 1])
    kw2_d = din("kw2", [128, 1])
    bfg_d = din("bfg", [128, 8])
    psc_d = din("psc", [128, 4])
    out_d = nc.dram_tensor("out", [NOWN * 128, D], F32, kind="ExternalOutput").ap()
    wu_s = nc.dram_tensor("wu_s", [D, 512], BF16, kind="Internal").ap()
    wo_s = nc.dram_tensor("wo_s", [D, D], BF16, kind="Internal").ap()
    wg_s = nc.dram_tensor("wg_s", [D, DFF], BF16, kind="Internal").ap()
    wp_s = nc.dram_tensor("wp_s", [D, DFF], BF16, kind="Internal").ap()
    wd_s = nc.dram_tensor("wd_s", [DFF, D], BF16, kind="Internal").ap()

    with ExitStack() as es:
        T = Tracker(nc, es)

        def stop_here(tag):
            if stop == tag:
                T.barrier()
                raise _Stop(nc)

        def sb(name, shape, dtype, st=None):
            return (st or es).enter_context(nc.sbuf_tensor(name, shape, dtype))

        ident = sb("ident", [128, 128], BF16)
        tri = sb("tri", [128, 128], BF16)
        bones = sb("bones", [128, 128], BF16)
        trif = sb("trif", [128, 128], F32)
        onesf = sb("onesf", [128, 128], F32)
        n1w = sb("n1w_sb", [128, 8], F32)
        n2w = sb("n2w_sb", [128, 8], F32)
        qw2 = sb("qw2_sb", [128, 1], F32)
        kw2 = sb("kw2_sb", [128, 1], F32)
        bfg = sb("bfg_sb", [128, 8], F32)
        psc = sb("psc_sb", [128, 4], F32)
        flagS = sb("flagS", [128, NV], F32)
        AT = sb("AT", [128, 4, NOWN * 128], BF16)
        PB = [es.enter_context(nc.psum_tensor("pb%d" % i, [128, 512], F32)) for i in range(8)]
        PBb = [p.bitcast(BF16) for p in PB]
        R_PB = [Res("pb%d" % i, excl=True) for i in range(8)]
        R_c = Res("consts")

        T.op("pool", lambda: nc.gpsimd.memset(ident[:], 1.0), writes=[R_c])
        T.op("pool", lambda: nc.gpsimd.affine_select(
            out=ident[:], in_=ident[:], pattern=[[1, 128]], compare_op=ALU.is_equal,
            fill=0.0, base=0, channel_multiplier=-1), reads=[R_c], writes=[R_c])
        T.op("pool", lambda: nc.gpsimd.memset(tri[:], 1.0))
        T.op("pool", lambda: nc.gpsimd.memset(trif[:], 1.0))
        T.op("pool", lambda: nc.gpsimd.memset(onesf[:], 1.0))
        T.op("pool", lambda: nc.gpsimd.memset(bones[:], 0.0), writes=[R_c])
        T.op("pool", lambda: nc.gpsimd.memset(bones[0:64, 0:64], 1.0), reads=[R_c], writes=[R_c])
        T.op("pool", lambda: nc.gpsimd.memset(bones[64:128, 64:128], 1.0), reads=[R_c], writes=[R_c])
        T.op("pool", lambda: nc.gpsimd.affine_select(
            out=tri[:], in_=tri[:], pattern=[[1, 128]], compare_op=ALU.is_ge,
            fill=0.0, base=0, channel_multiplier=-1), reads=[R_c], writes=[R_c])
        T.op("pool", lambda: nc.gpsimd.affine_select(
            out=trif[:], in_=trif[:], pattern=[[1, 128]], compare_op=ALU.is_ge,
            fill=0.0, base=0, channel_multiplier=-1), reads=[R_c], writes=[R_c])
        R_v = Res("vecs")
        for dst, src in ((n1w, n1w_d), (n2w, n2w_d), (qw2, qw2_d), (kw2, kw2_d),
                         (bfg, bfg_d), (psc, psc_d), (flagS, flag_d)):
            T.dma("sp", dst[:], src, writes=[R_v])
            R_v.w = None

        st_att = ExitStack()
        KT = sb("KT", [128, 2, NVC], BF16, st_att)
        VAf = sb("VAf", [128, NV * 260 + 64], BF16, st_att)
        VA = VAf[:, 0:NV * 260].rearrange("p (t h c) -> p t h c", t=NV, h=4)
        atmp = [sb("atmp%d" % i, [64, 512], BF16, st_att) for i in range(2)]
        R_atmp = [Res("atmp0"), Res("atmp1")]
        Wk = sb("Wk", [128, 8, 256], BF16, st_att)
        Wq = sb("Wq", [128, 8, 256], BF16, st_att)
        Wvf = sb("Wvf", [128, 8, 260], BF16, st_att)
        FG = sb("FG", [128, NV * 4], F32, st_att)
        EL = sb("EL", [128, NV * 4], F32, st_att)
        TOT = sb("TOT", [128, NV * 4], F32, st_att)
        PFX = sb("PFX", [128, (NV + 1) * 4], F32, st_att)
        CUMP = sb("CUMP", [128, NV * 4], F32, st_att)
        BIASJ = [sb("BIASJ%d" % i, [128, NV], F32, st_att) for i in range(2)]
        xs = [sb("xs%d" % i, [128, D], F32, st_att) for i in range(XR)]
        xn = sb("xn", [128, 4, D], BF16, st_att)
        xnB = sb("xnB", [128, 4, D], BF16, st_att)
        nT = [sb("nT%d" % i, [128, 8, 512], BF16, st_att) for i in range(2)]
        ssq = sb("ssq", [128, 4], F32, st_att)
        rs = sb("rs", [128, 4], F32, st_att)
        sqb = [sb("sqb%d" % i, [128, 512], BF16, st_att) for i in range(2)]
        rstd = [sb("rstd%d" % i, [128, 512], F32, st_att) for i in range(2)]
        junk = [sb("junk%d" % i, [128, D], BF16, st_att) for i in range(2)]
        QT = [sb("QT%d" % i, [128, 4, 512], BF16, st_att) for i in range(2)]
        NPT = 4
        PT = [sb("PT%d" % i, [128, 512], BF16, st_att) for i in range(NPT)]
        rec = sb("rec", [128, 512], F32, st_att)
        bcs = sb("bcs", [128, 512], F32, st_att)

        R_xs = [Res("xs%d" % i) for i in range(XR)]
        R_xn = [Res("xn%d" % i) for i in range(4)]
        R_xnB = [Res("xnB%d" % i) for i in range(4)]
        xsel = {"xn": xn, "R": R_xn}
        R_nT = [[Res("nT%d_%d" % (b, k)) for k in range(8)] for b in range(2)]
        R_ssq, R_rs, R_junk = Res("ssq"), Res("rs"), [Res("junk0"), Res("junk1")]
        R_sq = [Res("sq0"), Res("sq1")]
        R_rstd = [Res("rstd0"), Res("rstd1")]
        R_Wk, R_Wq, R_Wvf = Res("Wk"), Res("Wq"), Res("Wvf")
        R_QT = [Res("QT0"), Res("QT1")]
        R_PT = [Res("PT%d" % i) for i in range(NPT)]
        R_BJ = [Res("BJ0"), Res("BJ1")]
        R_rec, R_bcs = Res("rec"), Res("bcs")
        R_EL, R_TOT, R_PFX, R_CUMP = Res("EL"), Res("TOT"), Res("PFX"), Res("CUMP")

        R_wus, R_wos, R_wgs, R_wps, R_wds = (Res("wus"), Res("wos"), Res("wgs"),
                                              Res("wps"), Res("wds"))

        conv_list = [
            lambda: (T.dma("pool", wu_s, w_in[:, 1544:2056], writes=[R_wus]),
                     T.dma("pool", wo_s, w_out, writes=[R_wos])),
            lambda: T.dma("pool", wg_s.rearrange("r (a c) -> r a c", a=2),
                          w_gate.rearrange("r (a c) -> r a c", a=2), writes=[R_wgs]),
            lambda: T.dma("pool", wp_s.rearrange("r (a c) -> r a c", a=2),
                          w_up.rearrange("r (a c) -> r a c", a=2), writes=[R_wps]),
            lambda: T.dma("pool", wd_s, w_down, writes=[R_wds]),
        ]

        T.barrier()

        def norm_T(srcs, dst, dst_res, wvec, pbank, mode, evac_engs, part="all"):
            n = len(srcs)
            P = srcs[0][2]
            if part in ("all", "elem"):
                norm_elem(srcs, n, P, mode)
            if part in ("all", "tr"):
                norm_tr(n, P, dst, dst_res, wvec, pbank, evac_engs)

        def norm_elem(srcs, n, P, mode, part="all"):
            if part in ("all", "a"):
                norm_elem_a(srcs, n, P, mode)
            if part in ("all", "b"):
                norm_elem_b(srcs, n, P, mode)

        def norm_elem_a(srcs, n, P, mode):
            for i, (ap, res, _) in enumerate(srcs):
                if mode == "act":
                    T.op("act", lambda: nc.scalar.activation(
                        out=junk[i % 2][0:P, :], in_=ap, func=AF.Square, accum_out=ssq[0:P, i:i + 1]),
                        reads=[res], writes=[R_junk[i % 2]] + ([R_ssq] if i in (0, n - 1) else []))
                else:
                    T.op("dve", lambda: nc.vector.scalar_tensor_tensor(
                        out=junk[i % 2][0:P, :], in0=ap, scalar=1.0, in1=ap, op0=ALU.mult, op1=ALU.mult,
                        accum_out=ssq[0:P, i:i + 1]), reads=[res],
                        writes=[R_junk[i % 2]] + ([R_ssq] if i in (0, n - 1) else []))

        def norm_elem_b(srcs, n, P, mode):
            T.op("act", lambda: nc.scalar.activation(
                out=rs[0:P, 0:n], in_=ssq[0:P, 0:n], func=AF.Sqrt, bias=EPS, scale=1.0 / D),
                reads=[R_ssq], writes=[R_rs])
            T.op("dve", lambda: nc.vector.reciprocal(out=rs[0:P, 0:n], in_=rs[0:P, 0:n]),
                 reads=[R_rs], writes=[R_rs])
            for i, (ap, res, _) in enumerate(srcs):
                if mode == "act" and i % 2 == 0:
                    T.op("act", lambda: nc.scalar.activation(
                        out=xsel["xn"][0:P, i, :], in_=ap, func=AF.Copy, scale=rs[0:P, i:i + 1]),
                        reads=[res, R_rs], writes=[xsel["R"][i]])
                else:
                    T.op("dve", lambda: nc.vector.tensor_scalar(
                        out=xsel["xn"][0:P, i, :], in0=ap, scalar1=rs[0:P, i:i + 1], scalar2=None,
                        op0=ALU.mult), reads=[res, R_rs], writes=[xsel["R"][i]])

        def norm_tr(n, P, dst, dst_res, wvec, pbank, evac_engs, kcs=range(8)):
            ncol = n * P
            xcur, rxcur = xsel["xn"], xsel["R"]
            for kc in kcs:
                bk = pbank[kc % len(pbank)]
                ptv = PBb[bk]
                for i in range(n):
                    T.op("pe", lambda: nc.tensor.transpose(
                        out=ptv[:, i * P:(i + 1) * P], in_=xcur[0:P, i, kc * 128:(kc + 1) * 128],
                        identity=ident[0:P, 0:P]),
                        reads=[rxcur[i]], writes=[R_PB[bk]], sig=(i == n - 1))
                ee = evac_engs[kc % len(evac_engs)]
                if ee == "act":
                    T.op("act", lambda: nc.scalar.activation(
                        out=dst[:, kc, 0:ncol], in_=ptv[:, 0:ncol], func=AF.Copy,
                        scale=wvec[:, kc:kc + 1]), reads=[R_PB[bk]], writes=[dst_res[kc]])
                else:
                    T.op("dve", lambda: nc.vector.tensor_scalar(
                        out=dst[:, kc, 0:ncol], in0=ptv[:, 0:ncol], scalar1=wvec[:, kc:kc + 1],
                        scalar2=None, op0=ALU.mult), reads=[R_PB[bk]], writes=[dst_res[kc]])

        def proj_qknorm(W, R_W, p, src, src_res, ncol, wv2, out_ap, out_res, pk, pss, mode,
                        part="all"):
            if part in ("all", "mm"):
                proj_mm(W, R_W, p, src, src_res, ncol, pk)
            if part in ("all", "norm"):
                proj_norm(p, ncol, wv2, out_ap, out_res, pk, pss)

        def proj_mm(W, R_W, p, src, src_res, ncol, pk, kcs=range(8)):
            for kc in kcs:
                T.op("pe", lambda: nc.tensor.matmul(
                    PB[pk][:, 0:ncol], lhsT=W[:, kc, p * 128:(p + 1) * 128], rhs=src[:, kc, 0:ncol],
                    start=(kc == 0), stop=(kc == 7)),
                    reads=[R_W, src_res[kc]], writes=[R_PB[pk]], sig=(kc == 7))
            if 7 not in kcs:
                return
            T.op("act", lambda: nc.scalar.activation(
                out=sqb[p][:, 0:ncol], in_=PB[pk][:, 0:ncol], func=AF.Square),
                reads=[R_PB[pk]], writes=[R_sq[p]])

        def proj_norm(p, ncol, wv2, out_ap, out_res, pk, pss):
            T.op("pe", lambda: nc.tensor.matmul(
                PB[pss][:, 0:ncol], lhsT=bones[:], rhs=sqb[p][:, 0:ncol], start=True, stop=True),
                reads=[R_sq[p]], writes=[R_PB[pss]])
            T.op("act", lambda: nc.scalar.activation(
                out=rstd[p][:, 0:ncol], in_=PB[pss][:, 0:ncol], func=AF.Sqrt, bias=EPS,
                scale=1.0 / 64), reads=[R_PB[pss]], writes=[R_rstd[p]])
            T.op("dve", lambda: nc.vector.reciprocal(
                out=rstd[p][:, 0:ncol], in_=rstd[p][:, 0:ncol]),
                reads=[R_rstd[p]], writes=[R_rstd[p]])
            for (q0, q1, oap) in out_ap:
                T.op("dve", lambda: nc.vector.scalar_tensor_tensor(
                    out=oap, in0=PB[pk][q0:q1, 0:ncol], scalar=wv2[q0:q1, 0:1],
                    in1=rstd[p][q0:q1, 0:ncol], op0=ALU.mult, op1=ALU.mult),
                    reads=[R_PB[pk], R_rstd[p]], writes=out_res)

        xdma_state = {"next": 0}

        def issue_x(tile_rows, upto):
            while xdma_state["next"] < min(upto, len(tile_rows)):
                k = xdma_state["next"]
                r0 = tile_rows[k]
                T.dma("sp", xs[k % XR][:], xv[r0:r0 + 128, :], writes=[R_xs[k % XR]])
                xdma_state["next"] += 1

        for hp in range(2):
            kcol = 512 + hp * 256
            T.dma("pool", Wk[:], w_in[:, kcol:kcol + 256].rearrange("(k p) c -> p k c", p=128),
                  writes=[R_Wk])
            T.dma("pool", Wvf[:, :, 0:256],
                  w_in[:, 1024 + hp * 256:1024 + hp * 256 + 256].rearrange("(k p) c -> p k c", p=128),
                  writes=[R_Wvf])
            T.dma("pool", Wvf[:, :, 256:260],
                  w_in[:, 1536 + hp * 4:1536 + hp * 4 + 4].rearrange("(k p) c -> p k c", p=128),
                  writes=[R_Wvf])
            T.dma("pool", Wq[:], w_in[:, hp * 256:hp * 256 + 256].rearrange("(k p) c -> p k c", p=128),
                  writes=[R_Wq])
            for hh in range(4):
                T.op("pool", lambda: nc.gpsimd.tensor_copy(out=VA[:, :, hh, 64], in_=flagS[:, :]))
            if hp == 0:
                T.op("pool", lambda: nc.gpsimd.memset(VAf[:, NV * 260:NV * 260 + 64], 0.0))
                for qb in range(2):
                    T.op("pool", lambda: nc.gpsimd.memset(QT[qb][:], 0.0), writes=[R_QT[qb]])

            rows1 = [t * 128 for t in range(NV)]
            xdma_state["next"] = 0
            base_k = 0
            def p1_info(g):
                tiles = list(range(4 * g, min(4 * g + 4, NV)))
                srcs = [(xs[t % XR][:], R_xs[t % XR], 128) for t in tiles]
                return tiles, srcs

            def p1_sel(g):
                xsel["xn"], xsel["R"] = (xn, R_xn) if g % 2 == 0 else (xnB, R_xnB)

            def p1_elem(g):
                tiles, srcs = p1_info(g)
                issue_x(rows1, 4 * g + XR)
                p1_sel(g)
                norm_elem(srcs, len(srcs), 128, "act")

            def p1_tr_items(g):
                tiles, srcs = p1_info(g)

                def mk(kc):
                    def f():
                        p1_sel(g)
                        norm_tr(len(tiles), 128, nT[g % 2], R_nT[g % 2], n1w, [0, 1], ["dve", "act"],
                                kcs=[kc])
                    return f
                return [mk(kc) for kc in range(8)]

            def p1_kv_items(g):
                tiles, _ = p1_info(g)
                ncol = len(tiles) * 128
                b = g % 2

                def kmm(p):
                    return lambda: proj_mm(Wk, R_Wk, p, nT[b], R_nT[b], ncol, 2 + p)

                def vmm(i, t):
                    def f():
                        bk = 5 + (i % 2)
                        for kc in range(8):
                            T.op("pe", lambda: nc.tensor.matmul(
                                PB[bk][:, 0:260], lhsT=nT[b][:, kc, i * 128:(i + 1) * 128],
                                rhs=Wvf[:, kc, :], start=(kc == 0), stop=(kc == 7)),
                                reads=[R_Wvf, R_nT[b][kc]], writes=[R_PB[bk]], sig=(kc == 7))
                        T.op("dve", lambda: nc.vector.tensor_copy(
                            out=VA[:, t, :, 0:64],
                            in_=PB[bk][:, 0:256].rearrange("p (a b) -> p a b", a=4)),
                            reads=[R_PB[bk]])
                        T.op("dve", lambda: nc.vector.tensor_tensor(
                            out=FG[:, t * 4:(t + 1) * 4], in0=PB[bk][:, 256:260],
                            in1=bfg[:, hp * 4:(hp + 1) * 4], op=ALU.add), reads=[R_PB[bk]])
                    return f
                items = [kmm(0)]
                vi = [vmm(i, t) for i, t in enumerate(tiles)]
                items += vi[0:1] + [kmm(1)] + vi[1:]
                return items

            def p1_tail(g):
                tiles, _ = p1_info(g)
                ncol = len(tiles) * 128
                for p in range(2):
                    proj_norm(p, ncol, kw2,
                              [(0, 128, KT[:, p, tiles[0] * 128:tiles[0] * 128 + ncol])], [],
                              2 + p, (4, 7)[p])

            p1_elem(0)
            if NG > 1:
                p1_elem(1)
            for f in p1_tr_items(0):
                f()
            for g in range(NG):
                if g + 2 < NG:
                    p1_elem(g + 2)
                A = p1_tr_items(g + 1) if g + 1 < NG else []
                Bk = p1_kv_items(g)
                for k in range(max(len(A), len(Bk))):
                    if k < len(A):
                        A[k]()
                    if k < len(Bk):
                        Bk[k]()
                p1_tail(g)
            xsel["xn"], xsel["R"] = xn, R_xn
            T.barrier()
            stop_here("p1_%d" % hp)
            NC4 = NV * 4
            T.op("act", lambda: nc.scalar.activation(out=EL[:], in_=FG[:], func=AF.Exp, scale=-1.0),
                 writes=[R_EL])
            T.op("act", lambda: nc.scalar.activation(out=EL[:], in_=EL[:], func=AF.Ln, bias=1.0),
                 reads=[R_EL], writes=[R_EL])
            T.op("pe", lambda: nc.tensor.matmul(PB[2][:, 0:NC4], lhsT=trif[:], rhs=EL[:],
                                                start=True, stop=True),
                 reads=[R_EL], writes=[R_PB[2]])
            T.op("pe", lambda: nc.tensor.matmul(PB[3][:, 0:NC4], lhsT=onesf[:], rhs=EL[:],
                                                start=True, stop=True),
                 reads=[R_EL], writes=[R_PB[3]])
            T.op("dve", lambda: nc.vector.tensor_copy(out=TOT[:], in_=PB[3][:, 0:NC4]),
                 reads=[R_PB[3]], writes=[R_TOT])
            T.op("dve", lambda: nc.vector.memset(PFX[:, 0:4], 0.0), writes=[R_PFX])
            for t in range(NV):
                T.op("dve", lambda: nc.vector.tensor_tensor(
                    out=PFX[:, (t + 1) * 4:(t + 2) * 4], in0=PFX[:, t * 4:(t + 1) * 4],
                    in1=TOT[:, t * 4:(t + 1) * 4], op=ALU.add),
                    reads=[R_TOT, R_PFX], writes=[R_PFX])
            T.op("dve", lambda: nc.vector.tensor_tensor(
                out=CUMP[:], in0=PB[2][:, 0:NC4], in1=PFX[:, 0:NC4], op=ALU.add),
                reads=[R_PB[2], R_PFX], writes=[R_CUMP])
            T.barrier()
            stop_here("cum_%d" % hp)

            rows2 = [(2 * m + 2) * 128 for m in range(NOWN)]
            xdma_state["next"] = 0

            def q_sel(j):
                xsel["xn"], xsel["R"] = (xn, R_xn) if j % 2 == 0 else (xnB, R_xnB)

            def q_srcs(j):
                return [(xs[(4 * j + i) % XR][:], R_xs[(4 * j + i) % XR], 128) for i in range(4)]

            def q_elem(j):
                issue_x(rows2, 4 * j + XR)
                q_sel(j)
                norm_elem(q_srcs(j), 4, 128, "dve", part="a")

            def q_items(j):
                b = j % 2
                items = [None] * 6

                def eb():
                    q_sel(j)
                    norm_elem(q_srcs(j), 4, 128, "dve", part="b")
                items.append(eb)
                items += [None] * 3

                def trk(kc):
                    def f():
                        q_sel(j)
                        norm_tr(4, 128, nT[b], R_nT[b], n1w, [6], ["dve"], kcs=[kc])
                    return f
                items += [trk(kc) for kc in range(8)]
                pks = (7, 6)
                for p in range(2):
                    for hh in range(2):
                        items.append(lambda p=p, hh=hh: proj_mm(
                            Wq, R_Wq, p, nT[b], R_nT[b], 512, pks[p], kcs=range(4 * hh, 4 * hh + 4)))
                for p in range(2):
                    items.append(lambda p=p: proj_norm(
                        p, 512, qw2,
                        [(0, 64, QT[b][0:64, 2 * p, :]), (64, 128, QT[b][64:128, 2 * p + 1, :])],
                        [R_QT[b]], pks[p], 5))
                return items

            steps = []
            for j in range(nslot):
                for hl in range(4):
                    nfull = 8 * j + 2
                    seq = [(kt, 0, None) for kt in range(nfull)]
                    for k in range(7):
                        i0 = (k + 1) // 2
                        seq.append((nfull + k, i0 * 128, (i0 if k % 2 == 0 else None)))
                    for n_, (kt, c0, dg) in enumerate(seq):
                        steps.append(dict(j=j, hl=hl, kt=kt, c0=c0, dg=dg, first=(n_ == 0),
                                          last=(n_ == len(seq) - 1)))
            LA = 2
            q_elem(0)
            for it_ in q_items(0):
                if it_ is not None:
                    it_()
            qpend = []
            state = {"hidx": 0}

            def emit_qk(si, s):
                j, hl, kt, c0 = s["j"], s["hl"], s["kt"], s["c0"]
                p, half = hl // 2, hl % 2
                r0 = half * 64
                b = j % 2
                bk = si % 3
                if s["first"]:
                    hb = (j * 4 + hl) % 2
                    ntile = 8 * j + 9
                    T.op("dve", lambda: nc.vector.tensor_scalar(
                        out=BIASJ[hb][:, 0:ntile],
                        in0=CUMP[:, 0:ntile * 4].rearrange("p (t h) -> p t h", h=4)[:, :, hl],
                        scalar1=PFX[:, (8 * j + 6) * 4 + hl:(8 * j + 6) * 4 + hl + 1], scalar2=None,
                        op0=ALU.subtract), writes=[R_BJ[hb]])
                T.op("pe", lambda: nc.tensor.matmul(
                    PB[bk][:, c0:512], lhsT=KT[:, p, kt * 128:(kt + 1) * 128],
                    rhs=QT[b][:, hl, c0:512], start=True, stop=True),
                    reads=[R_QT[b]], writes=[R_PB[bk]])
                hb = (j * 4 + hl) % 2
                ps = si % NPT
                T.op("act", lambda: nc.scalar.activation(
                    out=PT[ps][:, c0:512], in_=PB[bk][:, c0:512], func=AF.Exp,
                    bias=BIASJ[hb][:, kt:kt + 1], scale=0.125),
                    reads=[R_PB[bk], R_BJ[hb]], writes=[R_PT[ps]])
                if s["dg"] is not None:
                    i0 = s["dg"]
                    T.op("dve", lambda: nc.vector.tensor_tensor(
                        out=PT[ps][:, i0 * 128:(i0 + 1) * 128], in0=PT[ps][:, i0 * 128:(i0 + 1) * 128],
                        in1=tri[:], op=ALU.mult), reads=[R_PT[ps]], writes=[R_PT[ps]])

            def emit_pv(si, s):
                j, hl, kt, c0 = s["j"], s["hl"], s["kt"], s["c0"]
                ps = si % NPT
                hidx = j * 4 + hl
                ob = 3 + (hidx % 2)
                T.op("pe", lambda: nc.tensor.matmul(
                    PB[ob][:, c0:512],
                    lhsT=VAf[:, kt * 260 + hl * 65:kt * 260 + hl * 65 + 128], rhs=PT[ps][:, c0:512],
                    start=s["first"], stop=s["last"]),
                    reads=[R_PT[ps]], writes=[R_PB[ob]], sig=True)
                if s["last"]:
                    rsr = 64
                    o0 = 0
                    T.op("dve", lambda: nc.vector.reciprocal(
                        out=rec[rsr:rsr + 1, :], in_=PB[ob][rsr:rsr + 1, :]),
                        reads=[R_PB[ob]], writes=[R_rec])
                    def fin(j=j, hl=hl, ob=ob, hidx=hidx, rsr=rsr, o0=o0):
                        T.op("pe", lambda: nc.tensor.matmul(
                            PB[5][o0:o0 + 64, :], lhsT=onesf[rsr:rsr + 1, 0:64],
                            rhs=rec[rsr:rsr + 1, :], start=True, stop=True),
                            reads=[R_rec], writes=[R_PB[5]])
                        T.op("dve", lambda: nc.vector.tensor_copy(
                            out=bcs[o0:o0 + 64, :], in_=PB[5][o0:o0 + 64, :]),
                            reads=[R_PB[5]], writes=[R_bcs])
                        if hp == 0:
                            T.op("dve", lambda: nc.vector.tensor_tensor(
                                out=AT[0:64, hl, j * 512:(j + 1) * 512], in0=PB[ob][0:64, :],
                                in1=bcs[0:64, :], op=ALU.mult),
                                reads=[R_PB[ob], R_bcs])
                        else:
                            ab = hidx % 2
                            T.op("dve", lambda: nc.vector.tensor_tensor(
                                out=atmp[ab][:, :], in0=PB[ob][0:64, :],
                                in1=bcs[0:64, :], op=ALU.mult),
                                reads=[R_PB[ob], R_bcs], writes=[R_atmp[ab]])
                            T.dma("sp", AT[64:128, hl, j * 512:(j + 1) * 512], atmp[ab][:, :],
                                  reads=[R_atmp[ab]])
                    fin_q.append([2, fin])

            ns = len(steps)
            fin_q = []
            for si in range(ns + LA):
                for fq in fin_q:
                    fq[0] -= 1
                while fin_q and fin_q[0][0] <= 0:
                    fin_q.pop(0)[1]()
                if si < ns:
                    s_ = steps[si]
                    if s_["first"] and s_["hl"] == 0:
                        while qpend:
                            it_ = qpend.pop(0)
                            if it_ is not None:
                                it_()
                        if s_["j"] + 1 < nslot:
                            q_elem(s_["j"] + 1)
                            qpend.extend(q_items(s_["j"] + 1))
                        if hp == 0 and s_["j"] >= 2 and conv_list:
                            conv_list.pop(0)()
                    emit_qk(si, s_)
                if si >= LA:
                    emit_pv(si - LA, steps[si - LA])
                if qpend:
                    it_ = qpend.pop(0)
                    if it_ is not None:
                        it_()
            while qpend:
                it_ = qpend.pop(0)
                if it_ is not None:
                    it_()
            while fin_q:
                fin_q.pop(0)[1]()
            while hp == 0 and conv_list:
                conv_list.pop(0)()
            xsel["xn"], xsel["R"] = xn, R_xn
            T.barrier()
            stop_here("p2_%d" % hp)

        st_att.close()

        st3 = ExitStack()
        H = [sb("H%d" % i, [128, D], F32, st3) for i in range(4)]
        XH = sb("XH", [64, D], F32, st3)
        xsF = [sb("xsF%d" % i, [128, D], F32, st3) for i in range(2)]
        HX = sb("HX", [128, NF * 512], BF16, st3)
        xnF = sb("xnF", [128, 4, D], BF16, st3)
        xnh = sb("xnh", [64, D], BF16, st3)
        nT3 = sb("nT3", [128, 8, 512], BF16, st3)
        nTa = sb("nTa", [128, 8, 512], BF16, st3)
        nTh = sb("nTh", [128, 8, 64], BF16, st3)
        U = sb("U", [128, 4, 144], F32, st3)
        UA = sb("UA", [128, 4, 144], F32, st3)
        UB = sb("UB", [128, 4, 144], F32, st3)
        PL = sb("PL", [128, 512], BF16, st3)
        PO = sb("PO", [128, 4, 512], BF16, st3)
        Wd = sb("Wd", [128, NF, D], BF16, st3)
        WuR = sb("WuR", [128, 8, 512], BF16, st3)
        NST = 3
        RING = [sb("ring%d" % i, [128, 4096], BF16, st3) for i in range(NST)]
        Wp = sb("Wp", [128, 4, 128], BF16, st3)
        sg = [sb("sg%d" % i, [128, 512], BF16, st3) for i in range(2)]
        ssq3 = sb("ssq3", [128, 4], F32, st3)
        rs3 = sb("rs3", [128, 4], F32, st3)
        ssqh = sb("ssqh", [64, 1], F32, st3)
        rsh = sb("rsh", [64, 1], F32, st3)

        R_H = [Res("H%d" % i) for i in range(4)]
        R_XH = Res("XH")
        R_xsF = [Res("xsF0"), Res("xsF1")]
        R_HX = [Res("HX%d" % f) for f in range(NF)]
        R_xnF = [Res("xnF%d" % i) for i in range(4)]
        R_xnh = Res("xnh")
        R_nT3 = [Res("nT3_%d" % k) for k in range(8)]
        R_nTa = [Res("nTa_%d" % k) for k in range(8)]
        R_nTh = [Res("nTh_%d" % k) for k in range(8)]
        R_U, R_UA, R_UB = Res("U"), Res("UA"), Res("UB")
        R_PL = Res("PL")
        R_PO = [Res("PO%d" % g) for g in range(4)]
        R_Wd, R_Wp, R_WuR = Res("Wd"), Res("Wp"), Res("WuR")
        R_RING = [Res("ring%d" % i) for i in range(NST)]
        R_sg = [Res("sg0"), Res("sg1")]
        R_ssq3, R_rs3, R_ssqh, R_rsh = Res("ssq3"), Res("rs3"), Res("ssqh"), Res("rsh")

        T.dma("sp", WuR[:], wu_s.rearrange("(k p) c -> p k c", p=128), reads=[R_wus], writes=[R_WuR])
        T.dma("sp", Wd[:], wd_s.rearrange("(f p) c -> p f c", p=128), reads=[R_wds], writes=[R_Wd])
        T.dma("pool", Wp[:], w_pool.rearrange("g c d -> c g d"), writes=[R_Wp])

        NGU = DFF // 256
        chunk_list = []
        for j in range(nslot):
            chunk_list.append(("woa", j))
            chunk_list.append(("wop", j))
            for c in range(NGU):
                chunk_list.append(("gu", c))
        ws = {"next": 0}

        def issue_chunk(k):
            kind, arg = chunk_list[k]
            st = k % NST
            rg, rr = RING[st], R_RING[st]
            if kind == "woa":
                T.dma("sp", rg[0:64, 0:4096].rearrange("p (k c) -> p k c", k=4),
                      wo_s[0:256, :].rearrange("(k p) c -> p k c", p=64), reads=[R_wos], writes=[rr])
                rr.w = None
                T.dma("sp", rg[64:128, 0:4096].rearrange("p (k c) -> p k c", k=4),
                      wo_s[256:512, :].rearrange("(k p) c -> p k c", p=64), reads=[R_wos], writes=[rr])
            elif kind == "wop":
                T.dma("sp", rg[:, 0:4096].rearrange("p (k c) -> p k c", k=4),
                      wo_s[512:1024, :].rearrange("(k p) c -> p k c", p=128), reads=[R_wos], writes=[rr])
            else:
                c = arg
                T.dma("sp", rg[:, 0:2048].rearrange("p (k c) -> p k c", k=8),
                      wg_s[:, c * 256:(c + 1) * 256].rearrange("(k p) c -> p k c", p=128),
                      reads=[R_wgs], writes=[rr])
                rr.w = None
                T.dma("sp", rg[:, 2048:4096].rearrange("p (k c) -> p k c", k=8),
                      wp_s[:, c * 256:(c + 1) * 256].rearrange("(k p) c -> p k c", p=128),
                      reads=[R_wps], writes=[rr])

        def stream_upto(k):
            while ws["next"] <= min(k, len(chunk_list) - 1):
                issue_chunk(ws["next"])
                ws["next"] += 1

        ck = {"i": 0}

        def next_chunk():
            k = ck["i"]
            ck["i"] += 1
            stream_upto(k + NST - 2)
            return RING[k % NST], R_RING[k % NST]

        def sq_rs(srcs, P, ssq_t, rs_t, R_sq_, R_rs_, xn_of, xn_res_of):
            n = len(srcs)
            for i, (ap, res) in enumerate(srcs):
                T.op("act", lambda: nc.scalar.activation(
                    out=xn_of(i), in_=ap, func=AF.Square, accum_out=ssq_t[0:P, i:i + 1]),
                    reads=[res], writes=xn_res_of(i) + ([R_sq_] if i in (0, n - 1) else []))
            T.op("act", lambda: nc.scalar.activation(
                out=rs_t[0:P, 0:n], in_=ssq_t[0:P, 0:n], func=AF.Sqrt, bias=EPS, scale=1.0 / D),
                reads=[R_sq_], writes=[R_rs_])
            T.op("dve", lambda: nc.vector.reciprocal(out=rs_t[0:P, 0:n], in_=rs_t[0:P, 0:n]),
                 reads=[R_rs_], writes=[R_rs_])
            for i, (ap, res) in enumerate(srcs):
                if i % 2 == 0:
                    T.op("act", lambda: nc.scalar.activation(
                        out=xn_of(i), in_=ap, func=AF.Copy, scale=rs_t[0:P, i:i + 1]),
                        reads=[res, R_rs_], writes=xn_res_of(i))
                else:
                    T.op("dve", lambda: nc.vector.tensor_scalar(
                        out=xn_of(i), in0=ap, scalar1=rs_t[0:P, i:i + 1], scalar2=None, op0=ALU.mult),
                        reads=[res, R_rs_], writes=xn_res_of(i))

        def tr3(n, P, xn_of, xn_res_of, dst, dst_res, wvec, kcs, pbank=(0, 1)):
            ncol = n * P
            for kc in kcs:
                bk = pbank[kc % 2]
                ptv = PBb[bk]
                for i in range(n):
                    T.op("pe", lambda: nc.tensor.transpose(
                        out=ptv[:, i * P:(i + 1) * P], in_=xn_of(i)[:, kc * 128:(kc + 1) * 128],
                        identity=ident[0:P, 0:P]),
                        reads=xn_res_of(i), writes=[R_PB[bk]], sig=(i == n - 1))
                if kc % 2 == 0:
                    T.op("dve", lambda: nc.vector.tensor_scalar(
                        out=dst[:, kc, 0:ncol], in0=ptv[:, 0:ncol], scalar1=wvec[:, kc:kc + 1],
                        scalar2=None, op0=ALU.mult), reads=[R_PB[bk]], writes=[dst_res[kc]])
                else:
                    T.op("act", lambda: nc.scalar.activation(
                        out=dst[:, kc, 0:ncol], in_=ptv[:, 0:ncol], func=AF.Copy,
                        scale=wvec[:, kc:kc + 1]), reads=[R_PB[bk]], writes=[dst_res[kc]])

        def xn3(i):
            return HX[:, i * 1024:(i + 1) * 1024]

        def xn3_res(i):
            return [R_HX[2 * i], R_HX[2 * i + 1]]

        def front_elem(j):
            for i in range(4):
                r0 = (2 * (4 * j + i) + 2) * 128
                T.dma("sp", XH[i * 16:(i + 1) * 16, :], xv[r0 - 16:r0, :], writes=[R_XH])
                if i < 3:
                    R_XH.w = None
            n = 4
            for i in range(n):
                r0 = (2 * (4 * j + i) + 2) * 128
                T.dma("sp", xsF[i % 2][:], xv[r0:r0 + 128, :], writes=[R_xsF[i % 2]])
                T.op("act", lambda: nc.scalar.activation(
                    out=xnF[:, i, :], in_=xsF[i % 2][:], func=AF.Square, accum_out=ssq3[:, i:i + 1]),
                    reads=[R_xsF[i % 2]], writes=[R_xnF[i]] + ([R_ssq3] if i in (0, n - 1) else []))
                if i >= 0:
                    pass
            T.op("act", lambda: nc.scalar.activation(
                out=rs3[:, 0:n], in_=ssq3[:, 0:n], func=AF.Sqrt, bias=EPS, scale=1.0 / D),
                reads=[R_ssq3], writes=[R_rs3])
            T.op("dve", lambda: nc.vector.reciprocal(out=rs3[:, 0:n], in_=rs3[:, 0:n]),
                 reads=[R_rs3], writes=[R_rs3])
            for i in range(n):
                r0 = (2 * (4 * j + i) + 2) * 128
                T.dma("sp", xsF[i % 2][:], xv[r0:r0 + 128, :], writes=[R_xsF[i % 2]])
                if i % 2 == 0:
                    T.op("act", lambda: nc.scalar.activation(
                        out=xnF[:, i, :], in_=xsF[i % 2][:], func=AF.Copy, scale=rs3[:, i:i + 1]),
                        reads=[R_xsF[i % 2], R_rs3], writes=[R_xnF[i]])
                else:
                    T.op("dve", lambda: nc.vector.tensor_scalar(
                        out=xnF[:, i, :], in0=xsF[i % 2][:], scalar1=rs3[:, i:i + 1], scalar2=None,
                        op0=ALU.mult), reads=[R_xsF[i % 2], R_rs3], writes=[R_xnF[i]])
            sq_rs([(XH[:], R_XH)], 64, ssqh, rsh, R_ssqh, R_rsh, lambda i: xnh[:], lambda i: [R_xnh])

        def front_items(j):
            items = []
            for kc in range(8):
                def ftr(kc=kc):
                    tr3(4, 128, lambda i: xnF[:, i, :], lambda i: [R_xnF[i]], nTa, R_nTa, n1w, [kc])
                    tr3(1, 64, lambda i: xnh[:], lambda i: [R_xnh], nTh, R_nTh, n1w, [kc])
                items.append(ftr)

            def uproj(g):
                def f():
                    for kc in range(8):
                        T.op("pe", lambda: nc.tensor.matmul(
                            PB[4][:, :], lhsT=WuR[:, kc, g * 128:(g + 1) * 128], rhs=nTa[:, kc, :],
                            start=(kc == 0), stop=(kc == 7)),
                            reads=[R_WuR, R_nTa[kc]], writes=[R_PB[4]], sig=(kc == 7))
                    for kc in range(8):
                        T.op("pe", lambda: nc.tensor.matmul(
                            PB[5][:, 0:64], lhsT=WuR[:, kc, g * 128:(g + 1) * 128], rhs=nTh[:, kc, :],
                            start=(kc == 0), stop=(kc == 7)),
                            reads=[R_WuR, R_nTh[kc]], writes=[R_PB[5]], sig=(kc == 7))
                    w = 2 << g
                    T.op("dve", lambda: nc.vector.tensor_copy(
                        out=U[:, :, 16:144], in_=PB[4][:, :].rearrange("p (a b) -> p a b", a=4)),
                        reads=[R_PB[4]], writes=[R_U])
                    T.op("dve", lambda: nc.vector.tensor_copy(
                        out=U[:, :, 0:16], in_=PB[5][:, 0:64].rearrange("p (a b) -> p a b", a=4)),
                        reads=[R_PB[5], R_U], writes=[R_U])
                    cur, rcur = U, R_U
                    sh, lo, bi = 1, 0, 0
                    bufs = [(UA, R_UA), (UB, R_UB)]
                    while sh < w:
                        nxt, rnxt = bufs[bi]
                        bi ^= 1
                        T.op("dve", lambda: nc.vector.tensor_tensor(
                            out=nxt[:, :, lo + sh:144], in0=cur[:, :, lo + sh:144],
                            in1=cur[:, :, lo:144 - sh], op=ALU.add), reads=[rcur], writes=[rnxt])
                        cur, rcur = nxt, rnxt
                        lo += sh
                        sh *= 2
                    T.op("dve", lambda: nc.vector.scalar_tensor_tensor(
                        out=PL[:, :].rearrange("p (a b) -> p a b", a=4), in0=cur[:, :, 16:144],
                        scalar=1.0 / w, in1=U[:, :, 16:144], op0=ALU.mult, op1=ALU.subtract),
                        reads=[rcur, R_U], writes=[R_PL])
                return f

            def wpool(g):
                def f():
                    T.op("pe", lambda: nc.tensor.matmul(
                        PB[5][:, :], lhsT=Wp[:, g, :], rhs=PL[:, :], start=True, stop=True),
                        reads=[R_Wp, R_PL], writes=[R_PB[5]])
                    T.op("dve", lambda: nc.vector.tensor_scalar(
                        out=PO[:, g, :], in0=PB[5][:, :], scalar1=psc[:, g:g + 1], scalar2=None,
                        op0=ALU.mult), reads=[R_PB[5]], writes=[R_PO[g]])
                return f
            items.append(uproj(0))
            items.append(None)
            for g in range(1, 4):
                items.append(lambda g=g: (wpool(g - 1)(), uproj(g)()))
                items.append(None)
            items.append(wpool(3))
            return items

        front_elem(0)
        for it_ in front_items(0):
            if it_ is not None:
                it_()

        def load_res_tile(j, i):
            r0 = (2 * (4 * j + i) + 2) * 128
            T.dma("sp", xsF[i % 2][:], xv[r0:r0 + 128, :], writes=[R_xsF[i % 2]])

        load_res_tile(0, 0)
        load_res_tile(0, 1)
        for j in range(nslot):
            rga, rra = next_chunk()
            rgp, rrp = next_chunk()
            WoA = rga[:, 0:4096].rearrange("p (k c) -> p k c", k=4)
            WoP = rgp[:, 0:4096].rearrange("p (k c) -> p k c", k=4)
            for i in range(4):
                tok0 = (4 * j + i) * 128
                for nh in range(2):
                    bk = 6 + ((2 * i + nh) % 2)
                    for c in range(4):
                        T.op("pe", lambda: nc.tensor.matmul(
                            PB[bk][:, :], lhsT=AT[:, c, tok0:tok0 + 128],
                            rhs=WoA[:, c, nh * 512:(nh + 1) * 512], start=(c == 0), stop=False),
                            reads=[rra], writes=[R_PB[bk]], sig=False)
                    for g in range(4):
                        T.op("pe", lambda: nc.tensor.matmul(
                            PB[bk][:, :], lhsT=PO[:, g, i * 128:(i + 1) * 128],
                            rhs=WoP[:, g, nh * 512:(nh + 1) * 512], start=False, stop=(g == 3)),
                            reads=[rrp, R_PO[g]], writes=[R_PB[bk]], sig=(g == 3))
                    T.op("dve", lambda: nc.vector.tensor_tensor(
                        out=H[i][:, nh * 512:(nh + 1) * 512], in0=PB[bk][:, :],
                        in1=xsF[i % 2][:, nh * 512:(nh + 1) * 512], op=ALU.add),
                        reads=[R_PB[bk], R_xsF[i % 2]], writes=[R_H[i]])
                if i + 2 < 4:
                    load_res_tile(j, i + 2)
                T.op("act", lambda: nc.scalar.activation(
                    out=xn3(i), in_=H[i][:], func=AF.Square, accum_out=ssq3[:, i:i + 1]),
                    reads=[R_H[i]], writes=xn3_res(i) + ([R_ssq3] if i in (0, 3) else []))
            T.op("act", lambda: nc.scalar.activation(
                out=rs3[:, 0:4], in_=ssq3[:, 0:4], func=AF.Sqrt, bias=EPS, scale=1.0 / D),
                reads=[R_ssq3], writes=[R_rs3])
            T.op("dve", lambda: nc.vector.reciprocal(out=rs3[:, 0:4], in_=rs3[:, 0:4]),
                 reads=[R_rs3], writes=[R_rs3])
            for i in range(4):
                if i % 2 == 0:
                    T.op("act", lambda: nc.scalar.activation(
                        out=xn3(i), in_=H[i][:], func=AF.Copy, scale=rs3[:, i:i + 1]),
                        reads=[R_H[i], R_rs3], writes=xn3_res(i))
                else:
                    T.op("dve", lambda: nc.vector.tensor_scalar(
                        out=xn3(i), in0=H[i][:], scalar1=rs3[:, i:i + 1], scalar2=None, op0=ALU.mult),
                        reads=[R_H[i], R_rs3], writes=xn3_res(i))
            tr3(4, 128, xn3, xn3_res, nT3, R_nT3, n2w, range(8))
            fitems = []
            if j + 1 < nslot:
                front_elem(j + 1)
                fitems = front_items(j + 1)
            for c in range(NGU):
                rg, rr = next_chunk()
                Wg_c = rg[:, 0:2048].rearrange("p (k c) -> p k c", k=8)
                Wu_c = rg[:, 2048:4096].rearrange("p (k c) -> p k c", k=8)
                for ff in range(2):
                    f = 2 * c + ff
                    bg = 2 + (f % 2)
                    bu = 6 + (f % 2)
                    for kc in range(8):
                        T.op("pe", lambda: nc.tensor.matmul(
                            PB[bg][:, :], lhsT=Wg_c[:, kc, ff * 128:(ff + 1) * 128], rhs=nT3[:, kc, :],
                            start=(kc == 0), stop=(kc == 7)),
                            reads=[rr, R_nT3[kc]], writes=[R_PB[bg]], sig=(kc == 7))
                    for kc in range(8):
                        T.op("pe", lambda: nc.tensor.matmul(
                            PB[bu][:, :], lhsT=Wu_c[:, kc, ff * 128:(ff + 1) * 128], rhs=nT3[:, kc, :],
                            start=(kc == 0), stop=(kc == 7)),
                            reads=[rr, R_nT3[kc]], writes=[R_PB[bu]], sig=(kc == 7))
                    T.op("act", lambda: nc.scalar.activation(
                        out=sg[f % 2][:, :], in_=PB[bg][:, :], func=AF.Silu),
                        reads=[R_PB[bg]], writes=[R_sg[f % 2]])
                    T.op("dve", lambda: nc.vector.tensor_tensor(
                        out=HX[:, f * 512:(f + 1) * 512], in0=PB[bu][:, :], in1=sg[f % 2][:, :],
                        op=ALU.mult), reads=[R_PB[bu], R_sg[f % 2]], writes=[R_HX[f]])
                    if fitems:
                        it_ = fitems.pop(0)
                        if it_ is not None:
                            it_()
            while fitems:
                it_ = fitems.pop(0)
                if it_ is not None:
                    it_()
            if j + 1 < nslot:
                stream_upto(ck["i"] + 1)
                load_res_tile(j + 1, 0)
                load_res_tile(j + 1, 1)
            for i in range(4):
                for nh in range(2):
                    bk = 4 + ((2 * i + nh) % 2)
                    for f in range(NF):
                        T.op("pe", lambda: nc.tensor.matmul(
                            PB[bk][:, :], lhsT=HX[:, f * 512 + i * 128:f * 512 + (i + 1) * 128],
                            rhs=Wd[:, f, nh * 512:(nh + 1) * 512], start=(f == 0), stop=(f == NF - 1)),
                            reads=[R_Wd, R_HX[f]], writes=[R_PB[bk]], sig=(f == NF - 1))
                    T.op("dve", lambda: nc.vector.tensor_tensor(
                        out=H[i][:, nh * 512:(nh + 1) * 512], in0=PB[bk][:, :],
                        in1=H[i][:, nh * 512:(nh + 1) * 512], op=ALU.add),
                        reads=[R_PB[bk], R_H[i]], writes=[R_H[i]])
                m = 4 * j + i
                T.dma("sp", out_d[m * 128:(m + 1) * 128, :], H[i][:], reads=[R_H[i]])
        T.barrier()
        st3.close()
    return nc


def make_core_inputs(x_b, meta, role, nslot, shared):
    NV = 8 * nslot + 1
    lead = 2 if role == 0 else 1
    nx = NV - lead
    xvv = np.zeros((NV * 128, D), np.float32)
    xvv[lead * 128 - 16:lead * 128] = meta
    xvv[lead * 128:] = x_b[:nx * 128]
    fl = np.zeros((NV * 128,), np.float32)
    fl[lead * 128 - 16:] = 1.0
    d = dict(shared)
    d["xv"] = xvv
    d["flag"] = np.ascontiguousarray(fl.reshape(NV, 128).T)
    return d


def make_shared(norm1_w, w_in, b_fgate, q_norm_w, k_norm_w, w_pool, pool_scale, w_out,
                norm2_w, w_gate, w_up, w_down):
    f = lambda a: np.ascontiguousarray(np.asarray(a, np.float32))
    return dict(
        w_in=f(w_in[0]), w_out=f(w_out[0]), w_gate=f(w_gate[0]), w_up=f(w_up[0]),
        w_down=f(w_down[0]), w_pool=f(w_pool[0]),
        n1w=f(norm1_w[0].reshape(8, 128).T), n2w=f(norm2_w[0].reshape(8, 128).T),
        qw2=f(np.concatenate([q_norm_w[0], q_norm_w[0]]).reshape(128, 1)),
        kw2=f(np.concatenate([k_norm_w[0], k_norm_w[0]]).reshape(128, 1)),
        bfg=f(np.broadcast_to(b_fgate[0][None, :], (128, 8))),
        psc=f(pool_scale[0].reshape(4, 128).T),
    )


def run(x, meta_tokens, shared, nslot, core_ids, stop=None):
    nc = build(nslot, stop)
    B = x.shape[0]
    in_maps = []
    for c in range(2 * B):
        in_maps.append(make_core_inputs(np.asarray(x[c // 2], np.float32),
                                        np.asarray(meta_tokens, np.float32), c % 2, nslot, shared))
    res = run_bass_kernel_spmd(nc, in_maps, core_ids=core_ids)
    S = 8 * nslot * 128
    out = np.zeros((B, S, D), np.float32)
    for c in range(2 * B):
        o = np.asarray(res.results[c]["out"]).reshape(4 * nslot, 128, D)
        out[c // 2].reshape(8 * nslot, 128, D)[(c % 2)::2] = o
    return out


def kernel(x, meta_tokens, norm1_w, w_in, b_fgate, q_norm_w, k_norm_w, w_pool, pool_scale,
           w_out, norm2_w, w_gate, w_up, w_down):
    shared = make_shared(norm1_w, w_in, b_fgate, q_norm_w, k_norm_w, w_pool, pool_scale, w_out,
                         norm2_w, w_gate, w_up, w_down)
    return run(np.asarray(x), np.asarray(meta_tokens), shared, 8, list(range(8)))
```
